# Optimizing a Trainium2 kernel written in Bass

```python
import math
import jax, jax.numpy as jnp
from jax import lax
import numpy as np

D_MODEL = 1024
BATCH = 8
SEQ = 2048
DEPTH = 4

D_RNN = D_MODEL
N_LRU_BLOCKS = 8
LRU_BLOCK = D_RNN // N_LRU_BLOCKS
RNN_CONV_WIDTH = 4
LRU_C = 8.0
HEAD_DIM = 128
HEADS_PER_GROUP = 4
DILATED_GROUPS = ((128, 1), (512, 4), (2048, 16))
N_GROUPS = len(DILATED_GROUPS)
N_ATT_HEADS = N_GROUPS * HEADS_PER_GROUP
ATT_WIDTH = N_ATT_HEADS * HEAD_DIM
ATT_OUT_WIDTH = HEADS_PER_GROUP * HEAD_DIM
BLOCK = 128
ROPE_THETA = 10000.0
D_FF = 3 * D_MODEL
FFN_CONV_WIDTH = 3
EPS = 1e-6
NEG_INF = -1e30
IN_SPLITS = (D_RNN, D_RNN, ATT_WIDTH, ATT_WIDTH, ATT_WIDTH, D_MODEL, D_MODEL)
N_IN = sum(IN_SPLITS)

kernel_name = "hybrid_rglru_dilated_attn_convffn"


def rms_norm(x, g):
    x32 = x.astype(jnp.float32)
    y = x32 * lax.rsqrt(jnp.mean(x32 * x32, axis=-1, keepdims=True) + EPS)
    return (y * g.astype(jnp.float32)).astype(x.dtype)


def causal_depthwise_conv(x, w, b):
    width = w.shape[0]
    s = x.shape[1]
    xp = jnp.pad(x, ((0, 0), (width - 1, 0), (0, 0)))
    out = b
    for i in range(width):
        out = out + xp[:, i:i + s] * w[i]
    return out


def block_diag_linear(x, w, b):
    bsz, s, c = x.shape
    nb = w.shape[0]
    y = jnp.einsum('bsni,nij->bsnj', x.reshape(bsz, s, nb, c // nb), w)
    return y.reshape(bsz, s, c) + b


def rg_lru(x, wa, ba, wx, bx, lam):
    r = jax.nn.sigmoid(block_diag_linear(x, wa, ba).astype(jnp.float32))
    i = jax.nn.sigmoid(block_diag_linear(x, wx, bx).astype(jnp.float32))
    log_a = -LRU_C * r * jax.nn.softplus(-lam.astype(jnp.float32))
    a = jnp.exp(log_a)
    mult = jnp.sqrt(-jnp.expm1(2.0 * log_a))
    u = mult * (i * x.astype(jnp.float32))

    def combine(left, right):
        a_l, b_l = left
        a_r, b_r = right
        return a_l * a_r, a_r * b_l + b_r

    _, h = lax.associative_scan(combine, (a, u), axis=1)
    return h.astype(x.dtype)


def rotary(x, positions):
    half = x.shape[-1] // 2
    inv_freq = ROPE_THETA ** (-jnp.arange(half, dtype=jnp.float32) / half)
    ang = positions.astype(jnp.float32)[..., None] * inv_freq
    cos = jnp.cos(ang)[:, :, None, :]
    sin = jnp.sin(ang)[:, :, None, :]
    x32 = x.astype(jnp.float32)
    x1, x2 = x32[..., :half], x32[..., half:]
    return jnp.concatenate([x1 * cos - x2 * sin, x2 * cos + x1 * sin], axis=-1).astype(x.dtype)


def dilated_window_attention(q, k, v, dilation, span):
    bsz, s, h, dh = q.shape
    n = s // dilation
    nb = -(-n // BLOCK)
    n_pad = nb * BLOCK

    def to_blocks(t):
        t = t.reshape(bsz, n, dilation, h, dh).transpose(0, 2, 1, 3, 4)
        t = jnp.pad(t, ((0, 0), (0, 0), (0, n_pad - n), (0, 0), (0, 0)))
        return t.reshape(bsz, dilation, nb, BLOCK, h, dh)

    def with_prev(t):
        prev = jnp.pad(t, ((0, 0), (0, 0), (1, 0), (0, 0), (0, 0), (0, 0)))[:, :, :-1]
        return jnp.concatenate([prev, t], axis=3)

    qb = to_blocks(q)
    kw = with_prev(to_blocks(k))
    vw = with_prev(to_blocks(v))
    scores = jnp.einsum('bgnqhd,bgnkhd->bgnhqk', qb, kw).astype(jnp.float32)
    qi = jnp.arange(BLOCK)[:, None]
    kj = jnp.arange(2 * BLOCK)[None, :]
    dist = qi + BLOCK - kj
    blk = jnp.arange(nb)[:, None, None]
    valid = (dist >= 0)[None] & (dist <= span)[None] & ((blk > 0) | (kj >= BLOCK)[None])
    scores = jnp.where(valid[None, None, :, None], scores, NEG_INF)
    m = jnp.max(scores, axis=-1, keepdims=True)
    p = jnp.exp(scores - m)
    den = jnp.sum(p, axis=-1, keepdims=True)
    out = jnp.einsum('bgnhqk,bgnkhd->bgnqhd', (p / den).astype(v.dtype), vw)
    lse = (m + jnp.log(den))[..., 0].transpose(0, 1, 2, 4, 3)
    out = out.reshape(bsz, dilation, n_pad, h, dh)[:, :, :n].transpose(0, 2, 1, 3, 4).reshape(bsz, s, h, dh)
    lse = lse.reshape(bsz, dilation, n_pad, h)[:, :, :n].transpose(0, 2, 1, 3).reshape(bsz, s, h)
    return out, lse


def dilated_attention_mixer(q, k, v, positions, q_g, k_g):
    bsz, s, _ = q.shape
    q = q.reshape(bsz, s, N_ATT_HEADS, HEAD_DIM)
    k = k.reshape(bsz, s, N_ATT_HEADS, HEAD_DIM)
    v = v.reshape(bsz, s, N_ATT_HEADS, HEAD_DIM)
    q = rotary(rms_norm(q, q_g), positions) * (HEAD_DIM ** -0.5)
    k = rotary(rms_norm(k, k_g), positions)
    outs, lses = [], []
    for gi, (window, dilation) in enumerate(DILATED_GROUPS):
        hs = slice(gi * HEADS_PER_GROUP, (gi + 1) * HEADS_PER_GROUP)
        o, l = dilated_window_attention(q[:, :, hs], k[:, :, hs], v[:, :, hs], dilation, window // dilation)
        outs.append(o)
        lses.append(l)
    wts = jax.nn.softmax(jnp.stack(lses, axis=0), axis=0)
    o = jnp.sum(wts[..., None] * jnp.stack(outs, axis=0).astype(jnp.float32), axis=0)
    return o.reshape(bsz, s, ATT_OUT_WIDTH).astype(q.dtype)


def setup_inputs(seed: int = 0) -> dict:
    key = jax.random.key(seed)
    ks = jax.random.split(key, 24)
    f32 = jnp.float32

    def nrm(k, shape, scale):
        return jax.random.normal(k, shape, f32) * scale

    x = jax.random.normal(ks[0], (BATCH, SEQ, D_MODEL), f32)
    offsets = jax.random.randint(ks[1], (BATCH, 1), 0, 1024, dtype=jnp.int32)
    positions = offsets + jnp.arange(SEQ, dtype=jnp.int32)[None, :]
    a_pow = jax.random.uniform(ks[2], (DEPTH, D_RNN), f32, 0.9, 0.999)
    a0 = a_pow ** (1.0 / LRU_C)
    lru_lambda = jnp.log(a0) - jnp.log1p(-a0)
    return {
        "x": x,
        "positions": positions,
        "ln1_g": 1.0 + nrm(ks[3], (DEPTH, D_MODEL), 0.02),
        "w_in": nrm(ks[4], (DEPTH, D_MODEL, N_IN), D_MODEL ** -0.5),
        "rnn_conv_w": nrm(ks[5], (DEPTH, RNN_CONV_WIDTH, D_RNN), RNN_CONV_WIDTH ** -0.5),
        "rnn_conv_b": nrm(ks[6], (DEPTH, D_RNN), 0.01),
        "lru_wa": nrm(ks[7], (DEPTH, N_LRU_BLOCKS, LRU_BLOCK, LRU_BLOCK), LRU_BLOCK ** -0.5),
        "lru_ba": nrm(ks[8], (DEPTH, D_RNN), 0.01),
        "lru_wx": nrm(ks[9], (DEPTH, N_LRU_BLOCKS, LRU_BLOCK, LRU_BLOCK), LRU_BLOCK ** -0.5),
        "lru_bx": nrm(ks[10], (DEPTH, D_RNN), 0.01),
        "lru_lambda": lru_lambda,
        "q_norm_g": 1.0 + nrm(ks[11], (DEPTH, N_ATT_HEADS, HEAD_DIM), 0.02),
        "k_norm_g": 1.0 + nrm(ks[12], (DEPTH, N_ATT_HEADS, HEAD_DIM), 0.02),
        "proj_rnn": nrm(ks[13], (DEPTH, D_RNN, D_MODEL), D_RNN ** -0.5),
        "proj_attn": nrm(ks[14], (DEPTH, ATT_OUT_WIDTH, D_MODEL), ATT_OUT_WIDTH ** -0.5),
        "w_out": nrm(ks[15], (DEPTH, D_MODEL, D_MODEL), D_MODEL ** -0.5),
        "ln2_g": 1.0 + nrm(ks[16], (DEPTH, D_MODEL), 0.02),
        "w_up": nrm(ks[17], (DEPTH, D_MODEL, 2 * D_FF), D_MODEL ** -0.5),
        "ffn_conv_w": nrm(ks[18], (DEPTH, FFN_CONV_WIDTH, 2 * D_FF), FFN_CONV_WIDTH ** -0.5),
        "ffn_conv_b": nrm(ks[19], (DEPTH, 2 * D_FF), 0.01),
        "w_down": nrm(ks[20], (DEPTH, D_FF, D_MODEL), D_FF ** -0.5),
    }


def reference(x, positions, ln1_g, w_in, rnn_conv_w, rnn_conv_b, lru_wa, lru_ba, lru_wx, lru_bx,
              lru_lambda, q_norm_g, k_norm_g, proj_rnn, proj_attn, w_out, ln2_g, w_up,
              ffn_conv_w, ffn_conv_b, w_down):
    cuts = np.cumsum(IN_SPLITS)[:-1].tolist()
    for l in range(DEPTH):
        h = rms_norm(x, ln1_g[l])
        z = h @ w_in[l]
        xr, gr, q, k, v, g_rnn, g_att = jnp.split(z, cuts, axis=-1)
        xr = causal_depthwise_conv(xr, rnn_conv_w[l], rnn_conv_b[l])
        hr = rg_lru(xr, lru_wa[l], lru_ba[l], lru_wx[l], lru_bx[l], lru_lambda[l])
        y_rnn = hr * jax.nn.gelu(gr)
        y_att = dilated_attention_mixer(q, k, v, positions, q_norm_g[l], k_norm_g[l])
        merged = jax.nn.sigmoid(g_rnn) * (y_rnn @ proj_rnn[l]) + jax.nn.sigmoid(g_att) * (y_att @ proj_attn[l])
        x = x + merged @ w_out[l]
        h2 = rms_norm(x, ln2_g[l])
        u = causal_depthwise_conv(h2 @ w_up[l], ffn_conv_w[l], ffn_conv_b[l])
        u_gate, u_val = jnp.split(u, 2, axis=-1)
        x = x + (jax.nn.gelu(u_gate) * u_val) @ w_down[l]
    return x
```

```python
from contextlib import ExitStack
import numpy as np
import concourse.bass as bass
import concourse.mybir as mybir
from concourse.bass_utils import run_bass_kernel_spmd

F32 = mybir.dt.float32
BF16 = mybir.dt.bfloat16
I32 = mybir.dt.int32
AF = mybir.ActivationFunctionType
ALU = mybir.AluOpType

DEPTH = 4
NT = 2048
PADC = 3
HW = 2052
EPS = 1e-6
NEG = -30000.0
SC = 128 ** -0.5
DIL = (1, 4, 16)
TWO_PI = 6.283185307179586
CW1 = 6.28125
CW2 = TWO_PI - CW1

V_G1, V_G2, V_CW, V_CB, V_BA, V_BX, V_LAM, V_GQ, V_GK, V_FW, V_FB, NV = 0, 8, 16, 48, 56, 64, 72, 80, 92, 104, 248, 296
D_HBA, D_HBX, D_C8, D_C8H, NDV = 0, 8, 16, 24, 32

ENGS = ['tensor', 'vector', 'scalar', 'gpsimd', 'sync']


class _Stream:
    def __init__(self, name):
        self.name = name
        self.items = []
        self.count = 0
        self.waited = {}


class Prog:
    def __init__(self, nc):
        self.nc = nc
        self.st = {e: _Stream(e) for e in ENGS}
        self.last_write = {}
        self.readers = {}
        self.slots = {}
        self.slot_eng = {}

    def _need(self, eng, reads, writes):
        need = {}

        def add(tok):
            if tok is None:
                return
            k, v, e = tok
            if e == eng and eng == 'tensor':
                return
            if need.get(k, 0) < v:
                need[k] = v

        for r in reads:
            add(self.last_write.get(r))
        for w in writes:
            add(self.last_write.get(w))
            for k, (v, e) in self.readers.get(w, {}).items():
                add((k, v, e))
        return need

    def _commit_waits(self, s, need):
        waits = [(k, v) for k, v in need.items() if s.waited.get(k, 0) < v]
        for k, v in waits:
            s.waited[k] = v
        return waits

    def _record(self, tok, reads, writes):
        k, v, e = tok
        for w in writes:
            self.last_write[w] = tok
            self.readers[w] = {}
        for r in reads:
            d = self.readers.setdefault(r, {})
            if d.get(k, (0, None))[0] < v:
                d[k] = (v, e)

    def op(self, eng, fn, reads=(), writes=()):
        s = self.st[eng]
        need = self._need(eng, reads, writes)
        waits = self._commit_waits(s, need)
        s.count += 1
        tok = ('E:' + eng, s.count, eng)
        s.items.append(('op', waits, fn, None))
        self._record(tok, reads, writes)
        return tok

    def dma(self, eng, slot, fn, reads=(), writes=()):
        s = self.st[eng]
        need = self._need(eng, reads, writes)
        prev = self.slots.get(slot, 0)
        if prev:
            k = 'D:' + slot
            if need.get(k, 0) < prev:
                need[k] = prev
        waits = self._commit_waits(s, need)
        total = prev + 16
        self.slots[slot] = total
        self.slot_eng[slot] = eng
        tok = ('D:' + slot, total, None)
        s.items.append(('dma', waits, fn, slot))
        self._record(tok, reads, writes)
        return tok

    def barrier(self, engs=('tensor', 'vector', 'scalar', 'sync')):
        need = {}
        for e in engs:
            if self.st[e].count:
                need['E:' + e] = self.st[e].count
        for slot, tot in self.slots.items():
            if self.slot_eng[slot] in engs:
                need['D:' + slot] = tot
        for e in engs:
            s = self.st[e]
            waits = self._commit_waits(s, dict(need))
            if waits:
                s.items.append(('wait', waits, None, None))

    def finish(self, eng='sync'):
        s = self.st[eng]
        need = {'D:' + slot: tot for slot, tot in self.slots.items()}
        for e in ENGS:
            if self.st[e].count:
                need['E:' + e] = self.st[e].count
        waits = self._commit_waits(s, need)
        s.items.append(('wait', waits, None, None))

    def emit(self, es):
        nc = self.nc
        sems = {}
        for e in ENGS:
            sems['E:' + e] = es.enter_context(nc.semaphore('s_' + e))
        for slot in self.slots:
            sems['D:' + slot] = es.enter_context(nc.semaphore('d_' + slot))
        with nc.Block() as block:
            for e in ENGS:
                items = self.st[e].items
                if not items:
                    continue

                def body(engh, items=items, e=e):
                    for kind, waits, fn, slot in items:
                        for k, v in waits:
                            engh.wait_ge(sems[k], v)
                        if kind == 'op':
                            fn(engh).then_inc(sems['E:' + e], 1)
                        elif kind == 'dma':
                            fn(engh).then_inc(sems['D:' + slot], 16)

                getattr(block, e)(body)


def build(L, dbg=None):
    nc = bass.Bass("TRN2", target_bir_lowering=False)
    x_d = nc.dram_tensor("x", [8, 128, NT], F32, kind="ExternalInput").ap()
    pos_d = nc.dram_tensor("pos", [1, NT], I32, kind="ExternalInput").ap()
    vec_d = nc.dram_tensor("vecs", [128, L * NV + 1], F32, kind="ExternalInput").ap()
    cst_d = nc.dram_tensor("cst", [128, 640], F32, kind="ExternalInput").ap()
    watt_d = nc.dram_tensor("w_att", [L * 12, 128, 3072], F32, kind="ExternalInput").ap()
    wrnn_d = nc.dram_tensor("w_rnn", [L * 8, 128, 2304], F32, kind="ExternalInput").ap()
    wmrg_d = nc.dram_tensor("w_mrg", [L * 8, 128, 3584], F32, kind="ExternalInput").ap()
    wout_d = nc.dram_tensor("w_out", [L * 2, 128, 4096], F32, kind="ExternalInput").ap()
    wup_d = nc.dram_tensor("w_up", [L * 24, 128, 2048], F32, kind="ExternalInput").ap()
    wdn_d = nc.dram_tensor("w_dn", [L * 16, 128, 1536], F32, kind="ExternalInput").ap()
    y_d = nc.dram_tensor("y", [8, 128, NT], F32, kind="ExternalOutput").ap()
    cs_d = nc.dram_tensor("cs_scr", [2, 128, NT], F32, kind="Internal").ap()

    with ExitStack() as es:
        xT = es.enter_context(nc.sbuf_tensor("xT", [128, 8, NT], F32))
        hpad = es.enter_context(nc.sbuf_tensor("hpad", [128, 8, HW], BF16))
        ybuf = es.enter_context(nc.sbuf_tensor("ybuf", [128, 12 * NT], BF16))
        vecs = es.enter_context(nc.sbuf_tensor("vecs_sb", [128, L * NV + 1], F32))
        dvec = es.enter_context(nc.sbuf_tensor("dvec", [128, L * NDV], F32))
        cst = es.enter_context(nc.sbuf_tensor("cst_sb", [128, 640], BF16))
        wsl = [es.enter_context(nc.sbuf_tensor("wsl%d" % i, [128, 4096], BF16)) for i in range(2)]
        ARENA_B = 40 * 1024
        arena = es.enter_context(nc.sbuf_tensor("arena", [128, ARENA_B // 2], BF16))
        ps = es.enter_context(nc.psum_tensor("ps", [128, 8, 512], F32))
        arena32 = arena.bitcast(F32)
        ybuf32 = ybuf.bitcast(F32)

        yatt = ybuf[:, 0:4 * NT].rearrange("p (c t) -> p c t", c=4)
        yrnn = ybuf[:, 4 * NT:12 * NT].rearrange("p (c t) -> p c t", c=8)
        YR_B = 4 * NT * 2

        ident = cst[:, 0:128]
        ones = cst[:, 128:256]
        PT = cst[:, 256:384]
        maskb = cst[:, 384:640]

        p = Prog(nc)
        state = {'ps': 0, 'w': 0}

        def psn():
            b = state['ps']
            state['ps'] = (b + 1) % 8
            return b

        def mm(out, lhsT, rhs, start, stop, reads, writes):
            p.op('tensor', lambda e: e.matmul(out, lhsT=lhsT, rhs=rhs, start=start, stop=stop), reads, writes)

        def act(out, in_, func, reads, writes, scale=None, bias=None):
            kw = {}
            if scale is not None:
                kw['scale'] = scale
            if bias is not None:
                kw['bias'] = bias
            p.op('scalar', lambda e: e.activation(out=out, in_=in_, func=func, **kw), reads, writes)

        def vts(out, in0, s1, s2, op0, op1, reads, writes):
            if op1 is None:
                p.op('vector', lambda e: e.tensor_scalar(out=out, in0=in0, scalar1=s1, scalar2=None, op0=op0), reads, writes)
            else:
                p.op('vector', lambda e: e.tensor_scalar(out=out, in0=in0, scalar1=s1, scalar2=s2, op0=op0, op1=op1), reads, writes)

        def vtt(out, in0, in1, op, reads, writes):
            p.op('vector', lambda e: e.tensor_tensor(out=out, in0=in0, in1=in1, op=op), reads, writes)

        def vstt(out, in0, scalar, in1, op0, op1, reads, writes):
            p.op('vector', lambda e: e.scalar_tensor_tensor(out=out, in0=in0, scalar=scalar, in1=in1, op0=op0, op1=op1), reads, writes)

        def vcopy(out, in_, reads, writes):
            p.op('vector', lambda e: e.tensor_copy(out=out, in_=in_), reads, writes)

        def wload(src, n):
            s = state['w'] % 2
            state['w'] += 1
            p.dma('gpsimd', 'w%d' % s, lambda e: e.dma_start(out=wsl[s][:, 0:n], in_=src), writes=[('w', s)])
            return s

        def a16(off, n):
            return arena[:, off // 2: off // 2 + n]

        def a32(off, n):
            return arena32[:, off // 4: off // 4 + n]

        def y16(off, n):
            return ybuf[:, off // 2: off // 2 + n]

        def y32(off, n):
            return ybuf32[:, off // 4: off // 4 + n]

        HALL = [('h', i) for i in range(4)]
        tcs = [slice(i * 512, (i + 1) * 512) for i in range(4)]

        def hsl(tc):
            return slice(PADC + tc * 512, PADC + (tc + 1) * 512)

        p.dma('sync', 'vecs', lambda e: e.dma_start(out=vecs[:], in_=vec_d), writes=['vecs'])
        p.dma('gpsimd', 'cst', lambda e: e.dma_start(out=cst[:], in_=cst_d), writes=['cst'])
        for c in range(8):
            p.dma('sync', 'xin%d' % c, lambda e, c=c: e.dma_start(out=xT[:, c, :], in_=x_d[c]),
                  writes=[('x', c, t) for t in range(4)])
        p.op('vector', lambda e: e.memset(hpad[:, :, 0:PADC], 0.0), [], ['hpz'])
        posi = arena.bitcast(I32)[:, 0:NT]
        ang = a32(8192, NT)
        t_a = a32(16384, NT)
        t_b = a32(24576, NT)
        t_k = arena.bitcast(I32)[:, 8192:8192 + NT]
        pos_b = bass.AP(pos_d.tensor, 0, [[0, 128], [1, NT]])
        p.dma('sync', 'pos', lambda e: e.dma_start(out=posi, in_=pos_b), writes=['posi'])
        vcopy(ang, posi, ['posi'], ['ang'])
        invf = vecs[:, L * NV:L * NV + 1]
        vts(ang, ang, invf, None, ALU.mult, None, ['ang', 'vecs'], ['ang'])
        for which, shift in ((0, np.pi / 2), (1, 0.0)):
            vts(t_a, ang, 1.0 / TWO_PI, shift / TWO_PI + 0.5, ALU.mult, ALU.add, ['ang'], ['t_a'])
            vcopy(t_k, t_a, ['t_a'], ['t_k'])
            vcopy(t_a, t_k, ['t_k'], ['t_a'])
            vts(t_b, ang, float(shift), None, ALU.add, None, ['ang'], ['t_b'])
            vstt(t_b, t_a, -CW1, t_b, ALU.mult, ALU.add, ['t_a', 't_b'], ['t_b'])
            vstt(t_b, t_a, -CW2, t_b, ALU.mult, ALU.add, ['t_a', 't_b'], ['t_b'])
            vts(t_a, t_b, -np.pi, TWO_PI, ALU.is_lt, ALU.mult, ['t_b'], ['t_a'])
            vtt(t_b, t_b, t_a, ALU.add, ['t_a', 't_b'], ['t_b'])
            vts(t_a, t_b, np.pi, -TWO_PI, ALU.is_gt, ALU.mult, ['t_b'], ['t_a'])
            vtt(t_b, t_b, t_a, ALU.add, ['t_a', 't_b'], ['t_b'])
            vts(t_b, t_b, -3.1415925, 3.1415925, ALU.max, ALU.min, ['t_b'], ['t_b'])
            act(t_b, t_b, AF.Sin, ['t_b'], ['t_b'])
            p.dma('sync', 'cs%d' % which, lambda e, which=which: e.dma_start(out=cs_d[which], in_=t_b),
                  reads=['t_b'], writes=[('csd', which)])
        for l in range(L):
            vb, db = l * NV, l * NDV
            vts(dvec[:, db + D_HBA:db + D_HBA + 8], vecs[:, vb + V_BA:vb + V_BA + 8], 0.5, None, ALU.mult, None, ['vecs'], ['dvec'])
            vts(dvec[:, db + D_HBX:db + D_HBX + 8], vecs[:, vb + V_BX:vb + V_BX + 8], 0.5, None, ALU.mult, None, ['vecs'], ['dvec'])
            act(dvec[:, db + D_C8:db + D_C8 + 8], vecs[:, vb + V_LAM:vb + V_LAM + 8], AF.Exp, ['vecs'], ['dvec'], scale=-1.0)
            act(dvec[:, db + D_C8:db + D_C8 + 8], dvec[:, db + D_C8:db + D_C8 + 8], AF.Ln, ['dvec'], ['dvec'], bias=1.0)
            vts(dvec[:, db + D_C8H:db + D_C8H + 8], dvec[:, db + D_C8:db + D_C8 + 8], -4.0, None, ALU.mult, None, ['dvec'], ['dvec'])
            vts(dvec[:, db + D_C8:db + D_C8 + 8], dvec[:, db + D_C8:db + D_C8 + 8], -8.0, None, ALU.mult, None, ['dvec'], ['dvec'])
        p.barrier()

        def norm_phase(gcol):
            sq = [a16(0, 512), a16(1024, 512)]
            lnv = [a32(2048, 512), a32(4096, 512)]
            rstd = [a32(6144, 512), a32(8192, 512)]
            for tc in range(4):
                b = psn()
                for c in range(8):
                    act(sq[c % 2], xT[:, c, tcs[tc]], AF.Square, [('x', c, tc)], [('sq', c % 2)])
                    mm(ps[:, b, :], ones, sq[c % 2], c == 0, c == 7, [('sq', c % 2), 'cst'], [('ps', b)])
                act(lnv[tc % 2], ps[:, b, :], AF.Ln, [('ps', b)], [('lnv', tc % 2)], scale=1.0 / 1024, bias=EPS)
                act(rstd[tc % 2], lnv[tc % 2], AF.Exp, [('lnv', tc % 2)], [('rstd', tc % 2)], scale=-0.5)
                for c in range(8):
                    vstt(hpad[:, c, hsl(tc)], xT[:, c, tcs[tc]], vecs[:, gcol + c:gcol + c + 1], rstd[tc % 2],
                         ALU.mult, ALU.mult, [('x', c, tc), ('rstd', tc % 2), 'vecs'], [('h', tc)])

        def attn_phase(l):
            vb = l * NV
            Qb = [y16(YR_B + 0, NT), y16(YR_B + 4096, NT)]
            Kb = [y16(YR_B + 8192, NT), y16(YR_B + 12288, NT)]
            Vb = [y16(YR_B + 16384, NT).rearrange("p (t d) -> p t d", t=16),
                  y16(YR_B + 20480, NT).rearrange("p (t d) -> p t d", t=16)]
            accn = y32(YR_B + 24576, NT)
            accd = a32(0, NT)
            cosT = a32(8192, NT)
            sinT = a32(16384, NT)
            sq = [a16(24576, 512), a16(25600, 512)]
            qn = [a16(26624, 512), a16(27648, 512)]
            lnv = [a32(28672, 512), a32(30720, 512)]
            t1 = [a32(32768, 512), a32(34816, 512)]
            t2 = a32(36864, 512)
            pt = [a16(38912, 256), a16(39424, 256), a16(39936, 256), a16(40448, 256)]
            p.dma('sync', 'csl0', lambda e: e.dma_start(out=cosT, in_=cs_d[0]), reads=[('csd', 0)], writes=['cos'])
            p.dma('sync', 'csl1', lambda e: e.dma_start(out=sinT, in_=cs_d[1]), reads=[('csd', 1)], writes=['sin'])
            cnt = {'u': 0, 'pt': 0}
            heads = [(j, g) for j in range(4) for g in range(3)]

            def proj(hi):
                j, g = heads[hi]
                h = 4 * g + j
                d = DIL[g]
                M = NT // d
                nb = M // 128
                hb = hi % 2
                s = wload(watt_d[l * 12 + hi], 3072)
                ws = wsl[s][:, 0:3072].rearrange("p (k n) -> p k n", k=8)
                for which, dst, gcol in ((0, Qb[hb], V_GQ), (1, Kb[hb], V_GK)):
                    dkey = ('Q' if which == 0 else 'K', hb)
                    for tc in range(4):
                        u = cnt['u'] % 2
                        cnt['u'] += 1
                        b = psn()
                        for k in range(8):
                            mm(ps[:, b, :], ws[:, k, which * 128:(which + 1) * 128], hpad[:, k, hsl(tc)], k == 0, k == 7,
                               [('w', s), ('h', tc)], [('ps', b)])
                        act(sq[u], ps[:, b, :], AF.Square, [('ps', b)], [('sq', u)])
                        b2 = psn()
                        mm(ps[:, b2, :], ones, sq[u], True, True, [('sq', u), 'cst'], [('ps', b2)])
                        act(lnv[u], ps[:, b2, :], AF.Ln, [('ps', b2)], [('lnv', u)], scale=1.0 / 128, bias=EPS)
                        act(lnv[u], lnv[u], AF.Exp, [('lnv', u)], [('lnv', u)], scale=-0.5)
                        vstt(qn[u], ps[:, b, :], vecs[:, vb + gcol + h:vb + gcol + h + 1], lnv[u], ALU.mult, ALU.mult,
                             [('ps', b), ('lnv', u), 'vecs'], [('qn', u)])
                        b3 = psn()
                        mm(ps[:, b3, :], PT, qn[u], True, True, [('qn', u), 'cst'], [('ps', b3)])
                        vtt(t1[u], qn[u], cosT[:, tcs[tc]], ALU.mult, [('qn', u), 'cos'], [('t1', u)])
                        vtt(t2, ps[:, b3, :], sinT[:, tcs[tc]], ALU.mult, [('ps', b3), 'sin'], ['t2'])
                        if d == 1:
                            o_ap = dst[:, tcs[tc]]
                            i0, i1 = t1[u], t2
                        else:
                            m0 = tc * 512 // d
                            o_ap = dst.rearrange("p (r m) -> p m r", r=d)[:, m0:m0 + 512 // d, :]
                            i0 = t1[u].rearrange("p (m r) -> p m r", r=d)
                            i1 = t2.rearrange("p (m r) -> p m r", r=d)
                        vtt(o_ap, i0, i1, ALU.add, [('t1', u), 't2'], [dkey])
                for ti in range(16):
                    r, bb = divmod(ti, nb)
                    if ti % 4 == 0:
                        b = psn()
                    st0 = PADC + r + d * bb * 128
                    for k in range(8):
                        mm(ps[:, b, (ti % 4) * 128:(ti % 4 + 1) * 128], hpad[:, k, st0:st0 + d * 127 + 1:d],
                           ws[:, k, 256:384], k == 0, k == 7, [('w', s)] + HALL, [('ps', b)])
                    if ti % 4 == 3:
                        act(Vb[hb][:, ti - 3:ti + 1, :], ps[:, b, :].rearrange("p (t d) -> p t d", t=4), AF.Copy,
                            [('ps', b)], [('V', hb)])

            def attn(hi):
                j, g = heads[hi]
                d = DIL[g]
                M = NT // d
                nb = M // 128
                hb = hi % 2
                Q, K, V = Qb[hb], Kb[hb], Vb[hb]
                rq = [('Q', hb), ('K', hb)]
                for qg in range(4):
                    bo = psn()
                    bd = psn()
                    for qi in range(4):
                        ti = qg * 4 + qi
                        r, bb = divmod(ti, nb)
                        qc = slice(ti * 128, ti * 128 + 128)
                        pc = slice((ti - 1) * 128, ti * 128)
                        oc = slice(qi * 128, qi * 128 + 128)
                        pi_ = cnt['pt'] % 4
                        cnt['pt'] += 1
                        bs = psn()
                        if bb > 0:
                            mm(ps[:, bs, 0:256], ident, maskb[:, 0:256], True, False, ['cst'], [('ps', bs)])
                            mm(ps[:, bs, 0:128], K[:, pc], Q[:, qc], False, False, rq, [('ps', bs)])
                            mm(ps[:, bs, 128:256], K[:, qc], Q[:, qc], False, True, rq, [('ps', bs)])
                            act(pt[pi_], ps[:, bs, 0:256], AF.Exp, [('ps', bs)], [('pt', pi_)], scale=SC)
                            mm(ps[:, bo, oc], V[:, ti - 1, :], pt[pi_][:, 0:128], True, False, [('V', hb), ('pt', pi_)], [('ps', bo)])
                            mm(ps[:, bo, oc], V[:, ti, :], pt[pi_][:, 128:256], False, True, [('V', hb), ('pt', pi_)], [('ps', bo)])
                            mm(ps[:, bd, oc], ones, pt[pi_][:, 0:128], True, False, ['cst', ('pt', pi_)], [('ps', bd)])
                            mm(ps[:, bd, oc], ones, pt[pi_][:, 128:256], False, True, ['cst', ('pt', pi_)], [('ps', bd)])
                        else:
                            mm(ps[:, bs, 0:128], ident, maskb[:, 128:256], True, False, ['cst'], [('ps', bs)])
                            mm(ps[:, bs, 0:128], K[:, qc], Q[:, qc], False, True, rq, [('ps', bs)])
                            act(pt[pi_][:, 0:128], ps[:, bs, 0:128], AF.Exp, [('ps', bs)], [('pt', pi_)], scale=SC)
                            mm(ps[:, bo, oc], V[:, ti, :], pt[pi_][:, 0:128], True, True, [('V', hb), ('pt', pi_)], [('ps', bo)])
                            mm(ps[:, bd, oc], ones, pt[pi_][:, 0:128], True, True, ['cst', ('pt', pi_)], [('ps', bd)])
                    for src_b, acc, akey in ((bo, accn, 'accn'), (bd, accd, 'accd')):
                        src = ps[:, src_b, :]
                        if g == 0:
                            dst = acc[:, qg * 512:(qg + 1) * 512]
                        elif g == 1:
                            dst = acc[:, qg:NT:4]
                        else:
                            dst = acc.rearrange("p (i r) -> p r i", r=16)[:, 4 * qg:4 * qg + 4, :]
                            src = src.rearrange("p (q i) -> p q i", q=4)
                        if g == 0:
                            vcopy(dst, src, [('ps', src_b)], [akey])
                        else:
                            vtt(dst, src, dst, ALU.add, [('ps', src_b), akey], [akey])
                if g == 2:
                    for tc in range(4):
                        u = cnt['u'] % 2
                        cnt['u'] += 1
                        act(lnv[u], accd[:, tcs[tc]], AF.Ln, ['accd'], [('lnv', u)])
                        act(lnv[u], lnv[u], AF.Exp, [('lnv', u)], [('lnv', u)], scale=-1.0)
                        vtt(yatt[:, j, tcs[tc]], accn[:, tcs[tc]], lnv[u], ALU.mult, ['accn', ('lnv', u)], [('yatt', j)])

            proj(0)
            for hi in range(12):
                if hi + 1 < 12:
                    proj(hi + 1)
                attn(hi)

        def rnn_phase(l):
            vb, db = l * NV, l * NDV
            xc = a32(0, NT)
            thr = a32(8192, NT)
            gg = a16(16384, NT)
            xcb = [a16(20480, 512), a16(21504, 512)]
            av = [a32(22528, 512), a32(24576, 512)]
            a2 = [a32(26624, 512), a32(28672, 512)]
            uu = [a32(30720, 512), a32(32768, 512)]
            hh = [a32(34816, 512), a32(36864, 512)]
            cnt = 0
            for c in range(8):
                s = wload(wrnn_d[l * 8 + c], 2304)
                wxg = wsl[s][:, 0:2048].rearrange("p (k n) -> p k n", k=8)
                wa = wsl[s][:, 2048:2176]
                wx = wsl[s][:, 2176:2304]
                for w in range(5):
                    c0 = 509 * w
                    wl = min(512, PADC + NT - c0)
                    nv = wl - 3
                    b = psn()
                    for k in range(8):
                        mm(ps[:, b, 0:wl], wxg[:, k, 0:128], hpad[:, k, c0:c0 + wl], k == 0, k == 7,
                           [('w', s), 'hpz'] + HALL, [('ps', b)])
                    act(xc[:, c0:c0 + nv], ps[:, b, 3:3 + nv], AF.Identity, [('ps', b), 'vecs'], ['xc'],
                        scale=vecs[:, vb + V_CW + 24 + c:vb + V_CW + 25 + c], bias=vecs[:, vb + V_CB + c:vb + V_CB + c + 1])
                    for i in range(3):
                        vstt(xc[:, c0:c0 + nv], ps[:, b, i:i + nv], vecs[:, vb + V_CW + 8 * i + c:vb + V_CW + 8 * i + c + 1],
                             xc[:, c0:c0 + nv], ALU.mult, ALU.add, [('ps', b), 'xc', 'vecs'], ['xc'])
                for tc in range(4):
                    u = tc % 2
                    vcopy(xcb[u], xc[:, tcs[tc]], ['xc'], [('xcb', u)])
                    b1 = psn()
                    mm(ps[:, b1, :], wa, xcb[u], True, True, [('w', s), ('xcb', u)], [('ps', b1)])
                    b2 = psn()
                    mm(ps[:, b2, :], wx, xcb[u], True, True, [('w', s), ('xcb', u)], [('ps', b2)])
                    act(thr[:, tcs[tc]], ps[:, b1, :], AF.Tanh, [('ps', b1), 'dvec'], [('thr', tc)], scale=0.5,
                        bias=dvec[:, db + D_HBA + c:db + D_HBA + c + 1])
                    act(uu[u], ps[:, b2, :], AF.Tanh, [('ps', b2), 'dvec'], [('uu', u)], scale=0.5,
                        bias=dvec[:, db + D_HBX + c:db + D_HBX + c + 1])
                    vstt(xc[:, tcs[tc]], uu[u], 1.0, xc[:, tcs[tc]], ALU.add, ALU.mult, [('uu', u), 'xc', ('xcb', u)], ['xc'])
                    b3 = psn()
                    for k in range(8):
                        mm(ps[:, b3, :], wxg[:, k, 128:256], hpad[:, k, hsl(tc)], k == 0, k == 7,
                           [('w', s), ('h', tc)], [('ps', b3)])
                    act(gg[:, tcs[tc]], ps[:, b3, :], AF.Gelu_apprx_tanh, [('ps', b3)], [('gg', tc)])
                for tc in range(4):
                    u = cnt % 2
                    cnt += 1
                    c8 = dvec[:, db + D_C8 + c:db + D_C8 + c + 1]
                    c8h = dvec[:, db + D_C8H + c:db + D_C8H + c + 1]
                    act(av[u], thr[:, tcs[tc]], AF.Exp, [('thr', tc), 'dvec'], [('av', u)], scale=c8h, bias=c8h)
                    act(a2[u], thr[:, tcs[tc]], AF.Exp, [('thr', tc), 'dvec'], [('a2', u)], scale=c8, bias=c8)
                    act(a2[u], a2[u], AF.Ln, [('a2', u)], [('a2', u)], scale=-1.0, bias=1.0)
                    act(a2[u], a2[u], AF.Exp, [('a2', u)], [('a2', u)], scale=0.5)
                    vstt(uu[u], xc[:, tcs[tc]], 0.5, a2[u], ALU.mult, ALU.mult, ['xc', ('a2', u)], [('uu', u)])
                    init = 0.0 if tc == 0 else hh[1 - u][:, 511:512]
                    rd = [('av', u), ('uu', u)] + ([] if tc == 0 else [('hh', 1 - u)])
                    p.op('vector', lambda e, u=u, init=init: e.tensor_tensor_scan(
                        out=hh[u], data0=av[u], data1=uu[u], initial=init, op0=ALU.mult, op1=ALU.add), rd, [('hh', u)])
                    vtt(yrnn[:, c, tcs[tc]], hh[u], gg[:, tcs[tc]], ALU.mult, [('hh', u), ('gg', tc)], [('yrnn', c)])

        def merge_phase(l):
            mg = a16(0, 4 * NT).rearrange("p (c t) -> p c t", c=4)
            th1 = [a32(16384, 512), a32(18432, 512)]
            th2 = [a32(20480, 512), a32(22528, 512)]
            m1 = [a32(24576, 512), a32(26624, 512)]
            cnt = 0
            YALL = [('yrnn', c) for c in range(8)]
            AALL = [('yatt', j) for j in range(4)]
            for half in range(2):
                for ci in range(4):
                    c = half * 4 + ci
                    s = wload(wmrg_d[l * 8 + c], 3584)
                    prn = wsl[s][:, 0:1024].rearrange("p (k n) -> p k n", k=8)
                    pat = wsl[s][:, 1024:1536].rearrange("p (k n) -> p k n", k=4)
                    wg = wsl[s][:, 1536:3584].rearrange("p (k n) -> p k n", k=8)
                    for tc in range(4):
                        u = cnt % 2
                        cnt += 1
                        bA, bB, bG1, bG2 = psn(), psn(), psn(), psn()
                        for k in range(8):
                            mm(ps[:, bG1, :], wg[:, k, 0:128], hpad[:, k, hsl(tc)], k == 0, k == 7, [('w', s), ('h', tc)], [('ps', bG1)])
                        for k in range(8):
                            mm(ps[:, bG2, :], wg[:, k, 128:256], hpad[:, k, hsl(tc)], k == 0, k == 7, [('w', s), ('h', tc)], [('ps', bG2)])
                        for k in range(8):
                            mm(ps[:, bA, :], prn[:, k, :], yrnn[:, k, tcs[tc]], k == 0, k == 7, [('w', s)] + YALL, [('ps', bA)])
                        for k in range(4):
                            mm(ps[:, bB, :], pat[:, k, :], yatt[:, k, tcs[tc]], k == 0, k == 3, [('w', s)] + AALL, [('ps', bB)])
                        act(th1[u], ps[:, bG1, :], AF.Tanh, [('ps', bG1)], [('th1', u)], scale=0.5)
                        act(th2[u], ps[:, bG2, :], AF.Tanh, [('ps', bG2)], [('th2', u)], scale=0.5)
                        vstt(m1[u], th1[u], 1.0, ps[:, bA, :], ALU.add, ALU.mult, [('th1', u), ('ps', bA)], [('m1', u)])
                        vstt(th2[u], th2[u], 1.0, ps[:, bB, :], ALU.add, ALU.mult, [('th2', u), ('ps', bB)], [('th2', u)])
                        vtt(mg[:, ci, tcs[tc]], m1[u], th2[u], ALU.add, [('m1', u), ('th2', u)], [('mg', ci)])
                s = wload(wout_d[l * 2 + half], 4096)
                wo = wsl[s][:, 0:4096].rearrange("p (c k n) -> p c k n", c=8, k=4)
                for c2 in range(8):
                    for tc in range(4):
                        b = psn()
                        for ci in range(4):
                            mm(ps[:, b, :], wo[:, c2, ci, :], mg[:, ci, tcs[tc]], ci == 0, ci == 3,
                               [('w', s)] + [('mg', i) for i in range(4)], [('ps', b)])
                        vstt(xT[:, c2, tcs[tc]], ps[:, b, :], 0.5, xT[:, c2, tcs[tc]], ALU.mult, ALU.add,
                             [('ps', b), ('x', c2, tc)], [('x', c2, tc)])

        def ffn_phase(l):
            vb = l * NV
            acth = ybuf[:, 0:12 * NT].rearrange("p (f t) -> p f t", f=12)
            ug = [a32(0, 512), a32(2048, 512)]
            uv = [a32(4096, 512), a32(6144, 512)]
            cnt = 0
            for half in range(2):
                for fi in range(12):
                    f = half * 12 + fi
                    s = wload(wup_d[l * 24 + f], 2048)
                    wu = wsl[s][:, 0:2048].rearrange("p (k n) -> p k n", k=8)
                    for w in range(5):
                        c0 = 1 + 510 * w
                        wl = min(512, PADC + NT - c0)
                        nv = wl - 2
                        u = cnt % 2
                        cnt += 1
                        bg, bv = psn(), psn()
                        for k in range(8):
                            mm(ps[:, bg, 0:wl], wu[:, k, 0:128], hpad[:, k, c0:c0 + wl], k == 0, k == 7,
                               [('w', s), 'hpz'] + HALL, [('ps', bg)])
                        for k in range(8):
                            mm(ps[:, bv, 0:wl], wu[:, k, 128:256], hpad[:, k, c0:c0 + wl], k == 0, k == 7,
                               [('w', s), 'hpz'] + HALL, [('ps', bv)])
                        for bank, dst, key, fc in ((bg, ug[u], ('ug', u), f), (bv, uv[u], ('uv', u), 24 + f)):
                            def wcol(i, fc=fc):
                                return vecs[:, vb + V_FW + 48 * i + fc:vb + V_FW + 48 * i + fc + 1]
                            act(dst[:, 0:nv], ps[:, bank, 2:2 + nv], AF.Identity, [('ps', bank), 'vecs'], [key],
                                scale=wcol(2), bias=vecs[:, vb + V_FB + fc:vb + V_FB + fc + 1])
                            for i in range(2):
                                vstt(dst[:, 0:nv], ps[:, bank, i:i + nv], wcol(i), dst[:, 0:nv], ALU.mult, ALU.add,
                                     [('ps', bank), key, 'vecs'], [key])
                        act(ug[u][:, 0:nv], ug[u][:, 0:nv], AF.Gelu_apprx_tanh, [('ug', u)], [('ug', u)])
                        vtt(acth[:, fi, 510 * w:510 * w + nv], ug[u][:, 0:nv], uv[u][:, 0:nv], ALU.mult,
                            [('ug', u), ('uv', u)], [('act', fi)])
                for c in range(8):
                    s = wload(wdn_d[(l * 2 + half) * 8 + c], 1536)
                    wd = wsl[s][:, 0:1536].rearrange("p (k n) -> p k n", k=12)
                    for tc in range(4):
                        b = psn()
                        for fi in range(12):
                            mm(ps[:, b, :], wd[:, fi, :], acth[:, fi, tcs[tc]], fi == 0, fi == 11,
                               [('w', s)] + [('act', i) for i in range(12)], [('ps', b)])
                        vtt(xT[:, c, tcs[tc]], ps[:, b, :], xT[:, c, tcs[tc]], ALU.add, [('ps', b), ('x', c, tc)], [('x', c, tc)])

        for l in range(L):
            norm_phase(l * NV + V_G1)
            p.barrier()
            attn_phase(l)
            p.barrier()
            rnn_phase(l)
            p.barrier()
            merge_phase(l)
            p.barrier()
            norm_phase(l * NV + V_G2)
            p.barrier()
            ffn_phase(l)
            p.barrier()
        for c in range(8):
            p.dma('sync', 'yout%d' % c, lambda e, c=c: e.dma_start(out=y_d[c], in_=xT[:, c, :]),
                  reads=[('x', c, t) for t in range(4)])
        p.finish('sync')
        p.emit(es)
    return nc


def _chunk_k(W):
    K, N = W.shape
    return W.reshape(K // 128, 128, N).transpose(1, 0, 2)


def _colvec(v, n):
    return np.asarray(v).reshape(n, 128).T


def _consts():
    c = np.zeros((128, 640), np.float32)
    c[:, 0:128] = np.eye(128, dtype=np.float32)
    c[:, 128:256] = 1.0
    for m in range(128):
        if m < 64:
            c[m + 64, 256 + m] = -1.0
        else:
            c[m - 64, 256 + m] = 1.0
    i = np.arange(128)[:, None]
    j = np.arange(128)[None, :]
    c[:, 384:512] = np.where(i >= j, 0.0, NEG)
    c[:, 512:640] = np.where(i <= j, 0.0, NEG)
    return c


def _prep_weights(L, w_in, lru_wa, lru_wx, proj_rnn, proj_attn, w_out, w_up, w_down):
    watt = np.empty((L * 12, 128, 3072), np.float32)
    wrnn = np.empty((L * 8, 128, 2304), np.float32)
    wmrg = np.empty((L * 8, 128, 3584), np.float32)
    wout = np.empty((L * 2, 128, 4096), np.float32)
    wup = np.empty((L * 24, 128, 2048), np.float32)
    wdn = np.empty((L * 16, 128, 1536), np.float32)
    for l in range(L):
        wi = w_in[l]
        for j in range(4):
            for g in range(3):
                h = 4 * g + j
                cols = np.concatenate([wi[:, 2048 + h * 128:2048 + (h + 1) * 128],
                                       wi[:, 3584 + h * 128:3584 + (h + 1) * 128],
                                       wi[:, 5120 + h * 128:5120 + (h + 1) * 128]], axis=1)
                watt[l * 12 + j * 3 + g] = _chunk_k(cols).reshape(128, 3072)
        for c in range(8):
            cols = np.concatenate([wi[:, c * 128:(c + 1) * 128], wi[:, 1024 + c * 128:1024 + (c + 1) * 128]], axis=1)
            wrnn[l * 8 + c, :, 0:2048] = _chunk_k(cols).reshape(128, 2048)
            wrnn[l * 8 + c, :, 2048:2176] = lru_wa[l, c]
            wrnn[l * 8 + c, :, 2176:2304] = lru_wx[l, c]
            wmrg[l * 8 + c, :, 0:1024] = _chunk_k(proj_rnn[l][:, c * 128:(c + 1) * 128]).reshape(128, 1024)
            wmrg[l * 8 + c, :, 1024:1536] = _chunk_k(proj_attn[l][:, c * 128:(c + 1) * 128]).reshape(128, 512)
            gcols = np.concatenate([wi[:, 6656 + c * 128:6656 + (c + 1) * 128], wi[:, 7680 + c * 128:7680 + (c + 1) * 128]], axis=1)
            wmrg[l * 8 + c, :, 1536:3584] = _chunk_k(gcols).reshape(128, 2048)
        for half in range(2):
            blk = w_out[l][half * 512:(half + 1) * 512, :]
            a = blk.reshape(4, 128, 8, 128).transpose(1, 2, 0, 3)
            wout[l * 2 + half] = a.reshape(128, 4096)
        for f in range(24):
            cols = np.concatenate([w_up[l][:, f * 128:(f + 1) * 128], w_up[l][:, 3072 + f * 128:3072 + (f + 1) * 128]], axis=1)
            wup[l * 24 + f] = _chunk_k(cols).reshape(128, 2048)
        for half in range(2):
            for c in range(8):
                blk = w_down[l][half * 1536:(half + 1) * 1536, c * 128:(c + 1) * 128]
                wdn[(l * 2 + half) * 8 + c] = blk.reshape(12, 128, 128).transpose(1, 0, 2).reshape(128, 1536)
    return watt, wrnn, wmrg, wout, wup, wdn


def _prep_vecs(L, ln1_g, ln2_g, rnn_conv_w, rnn_conv_b, lru_ba, lru_bx, lru_lambda, q_norm_g, k_norm_g, ffn_conv_w, ffn_conv_b):
    out = np.zeros((128, L * NV + 1), np.float32)
    for l in range(L):
        parts = [_colvec(ln1_g[l], 8), _colvec(ln2_g[l], 8)]
        parts += [_colvec(rnn_conv_w[l][i], 8) for i in range(4)]
        parts += [_colvec(rnn_conv_b[l], 8), _colvec(lru_ba[l], 8), _colvec(lru_bx[l], 8), _colvec(lru_lambda[l], 8)]
        parts += [np.asarray(q_norm_g[l]).T, np.asarray(k_norm_g[l]).T]
        parts += [_colvec(ffn_conv_w[l][i], 48) for i in range(3)]
        parts += [_colvec(ffn_conv_b[l], 48)]
        out[:, l * NV:(l + 1) * NV] = np.concatenate(parts, axis=1)
    half = 64
    inv_freq = (np.float32(10000.0) ** (-np.arange(half, dtype=np.float32) / np.float32(half))).astype(np.float32)
    out[:, L * NV] = np.concatenate([inv_freq, inv_freq])
    return out


_NC_CACHE = {}


def _get_nc(L):
    if L not in _NC_CACHE:
        _NC_CACHE[L] = build(L)
    return _NC_CACHE[L]


def run_layers(x, positions, L, layer0, params, core_ids=None):
    B = x.shape[0]
    sl = slice(layer0, layer0 + L)
    f = lambda a: np.ascontiguousarray(np.asarray(a, dtype=np.float32)[sl])
    watt, wrnn, wmrg, wout, wup, wdn = _prep_weights(L, f(params['w_in']), f(params['lru_wa']), f(params['lru_wx']),
                                                    f(params['proj_rnn']), f(params['proj_attn']), f(params['w_out']),
                                                    f(params['w_up']), f(params['w_down']))
    vecs = _prep_vecs(L, f(params['ln1_g']), f(params['ln2_g']), f(params['rnn_conv_w']), f(params['rnn_conv_b']),
                      f(params['lru_ba']), f(params['lru_bx']), f(params['lru_lambda']), f(params['q_norm_g']),
                      f(params['k_norm_g']), f(params['ffn_conv_w']), f(params['ffn_conv_b']))
    cst = _consts()
    nc = _get_nc(L)
    in_maps = []
    for b in range(B):
        xt = np.ascontiguousarray(np.asarray(x[b], np.float32).T).reshape(8, 128, NT)
        in_maps.append({"x": xt, "pos": np.ascontiguousarray(np.asarray(positions[b], np.int32).reshape(1, NT)),
                        "vecs": vecs, "cst": cst, "w_att": watt, "w_rnn": wrnn, "w_mrg": wmrg, "w_out": wout,
                        "w_up": wup, "w_dn": wdn})
    res = run_bass_kernel_spmd(nc, in_maps, core_ids=list(range(B)) if core_ids is None else core_ids)
    out = np.empty((B, NT, 1024), np.float32)
    for b in range(B):
        out[b] = np.asarray(res.results[b]["y"], np.float32).reshape(1024, NT).T
    return out


def kernel(x, positions, ln1_g, w_in, rnn_conv_w, rnn_conv_b, lru_wa, lru_ba, lru_wx, lru_bx,
           lru_lambda, q_norm_g, k_norm_g, proj_rnn, proj_attn, w_out, ln2_g, w_up,
           ffn_conv_w, ffn_conv_b, w_down):
    params = dict(ln1_g=ln1_g, w_in=w_in, rnn_conv_w=rnn_conv_w, rnn_conv_b=rnn_conv_b, lru_wa=lru_wa, lru_ba=lru_ba,
                  lru_wx=lru_wx, lru_bx=lru_bx, lru_lambda=lru_lambda, q_norm_g=q_norm_g, k_norm_g=k_norm_g,
                  proj_rnn=proj_rnn, proj_attn=proj_attn, w_out=w_out, ln2_g=ln2_g, w_up=w_up,
                  ffn_conv_w=ffn_conv_w, ffn_conv_b=ffn_conv_b, w_down=w_down)
    return run_layers(np.asarray(x), np.asarray(positions), DEPTH, 0, params)
```

```python
from contextlib import ExitStack
import numpy as np
import concourse.bass as bass
import concourse.mybir as mybir
from concourse.bass_utils import run_bass_kernel_spmd

F32 = mybir.dt.float32
BF16 = mybir.dt.bfloat16
I32 = mybir.dt.int32
AF = mybir.ActivationFunctionType
ALU = mybir.AluOpType

DEPTH = 4
NT = 2048
PADC = 3
HW = 2052
EPS = 1e-6
NEG = -30000.0
SC = 128 ** -0.5
DIL = (1, 4, 16)
TWO_PI = 6.283185307179586
CW1 = 6.28125
CW2 = TWO_PI - CW1

V_G1, V_G2, V_CW, V_CB, V_BA, V_BX, V_LAM, V_GQ, V_GK, V_FW, V_FB, NV = 0, 8, 16, 48, 56, 64, 72, 80, 92, 104, 248, 296
D_HBA, D_HBX, D_C8, D_C8H, NDV = 0, 8, 16, 24, 32

ENGS = ['tensor', 'vector', 'scalar', 'gpsimd', 'sync']


class _Stream:
    def __init__(self, name):
        self.name = name
        self.items = []
        self.count = 0
        self.waited = {}


class Prog:
    def __init__(self, nc):
        self.nc = nc
        self.st = {e: _Stream(e) for e in ENGS}
        self.last_write = {}
        self.readers = {}
        self.slots = {}
        self.slot_eng = {}
        self.tag = ''
        self.tags = {e: [] for e in ENGS}

    def _need(self, eng, reads, writes):
        need = {}

        def add(tok):
            if tok is None:
                return
            k, v, e = tok
            if e == eng and eng == 'tensor':
                return
            if need.get(k, 0) < v:
                need[k] = v

        for r in reads:
            add(self.last_write.get(r))
        for w in writes:
            add(self.last_write.get(w))
            for k, (v, e) in self.readers.get(w, {}).items():
                add((k, v, e))
        return need

    def _commit_waits(self, s, need):
        waits = [(k, v) for k, v in need.items() if s.waited.get(k, 0) < v]
        for k, v in waits:
            s.waited[k] = v
        return waits

    def _record(self, tok, reads, writes):
        k, v, e = tok
        for w in writes:
            self.last_write[w] = tok
            self.readers[w] = {}
        for r in reads:
            d = self.readers.setdefault(r, {})
            if d.get(k, (0, None))[0] < v:
                d[k] = (v, e)

    def op(self, eng, fn, reads=(), writes=()):
        s = self.st[eng]
        need = self._need(eng, reads, writes)
        waits = self._commit_waits(s, need)
        s.count += 1
        tok = ('E:' + eng, s.count, eng)
        s.items.append(('op', waits, fn, None))
        self.tags[eng].append(self.tag)
        self._record(tok, reads, writes)
        return tok

    def dma(self, eng, slot, fn, reads=(), writes=()):
        s = self.st[eng]
        need = self._need(eng, reads, writes)
        prev = self.slots.get(slot, 0)
        if prev:
            k = 'D:' + slot
            if need.get(k, 0) < prev:
                need[k] = prev
        waits = self._commit_waits(s, need)
        total = prev + 16
        self.slots[slot] = total
        self.slot_eng[slot] = eng
        tok = ('D:' + slot, total, None)
        s.items.append(('dma', waits, fn, slot))
        self._record(tok, reads, writes)
        return tok

    def barrier(self, engs=('tensor', 'vector', 'scalar', 'gpsimd', 'sync')):
        need = {}
        for e in engs:
            if self.st[e].count:
                need['E:' + e] = self.st[e].count
        for slot, tot in self.slots.items():
            if self.slot_eng[slot] == 'sync':
                need['D:' + slot] = tot
        for e in engs:
            s = self.st[e]
            waits = self._commit_waits(s, dict(need))
            if waits:
                s.items.append(('wait', waits, None, None))

    def finish(self, eng='sync'):
        s = self.st[eng]
        need = {'D:' + slot: tot for slot, tot in self.slots.items()}
        for e in ENGS:
            if self.st[e].count:
                need['E:' + e] = self.st[e].count
        waits = self._commit_waits(s, need)
        s.items.append(('wait', waits, None, None))

    def emit(self, es):
        nc = self.nc
        sems = {}
        for e in ENGS:
            sems['E:' + e] = es.enter_context(nc.semaphore('s_' + e))
        for slot in self.slots:
            sems['D:' + slot] = es.enter_context(nc.semaphore('d_' + slot))
        with nc.Block() as block:
            for e in ENGS:
                items = self.st[e].items
                if not items:
                    continue

                def body(engh, items=items, e=e):
                    for kind, waits, fn, slot in items:
                        for k, v in waits:
                            engh.wait_ge(sems[k], v)
                        if kind == 'op':
                            fn(engh).then_inc(sems['E:' + e], 1)
                        elif kind == 'dma':
                            fn(engh).then_inc(sems['D:' + slot], 16)

                getattr(block, e)(body)


def build(L, dbg=None):
    nc = bass.Bass("TRN2", target_bir_lowering=False)
    x_d = nc.dram_tensor("x", [8, 128, NT], F32, kind="ExternalInput").ap()
    pos_d = nc.dram_tensor("pos", [1, NT], I32, kind="ExternalInput").ap()
    vec_d = nc.dram_tensor("vecs", [128, L * NV + 1], F32, kind="ExternalInput").ap()
    cst_d = nc.dram_tensor("cst", [128, 640], F32, kind="ExternalInput").ap()
    watt_d = nc.dram_tensor("w_att", [L * 12, 128, 3072], F32, kind="ExternalInput").ap()
    wrnn_d = nc.dram_tensor("w_rnn", [L * 8, 128, 2304], F32, kind="ExternalInput").ap()
    wmrg_d = nc.dram_tensor("w_mrg", [L * 8, 128, 3584], F32, kind="ExternalInput").ap()
    wout_d = nc.dram_tensor("w_out", [L * 2, 128, 4096], F32, kind="ExternalInput").ap()
    wup_d = nc.dram_tensor("w_up", [L * 24, 128, 2048], F32, kind="ExternalInput").ap()
    wdn_d = nc.dram_tensor("w_dn", [L * 16, 128, 1536], F32, kind="ExternalInput").ap()
    y_d = nc.dram_tensor("y", [8, 128, NT], F32, kind="ExternalOutput").ap()
    cs_d = nc.dram_tensor("cs_scr", [2, 128, NT], F32, kind="Internal").ap()

    with ExitStack() as es:
        xT = es.enter_context(nc.sbuf_tensor("xT", [128, 8, NT], F32))
        hpad = es.enter_context(nc.sbuf_tensor("hpad", [128, 8, HW], BF16))
        ybuf = es.enter_context(nc.sbuf_tensor("ybuf", [128, 12 * NT], BF16))
        vecs = es.enter_context(nc.sbuf_tensor("vecs_sb", [128, L * NV + 1], F32))
        dvec = es.enter_context(nc.sbuf_tensor("dvec", [128, L * NDV], F32))
        cst = es.enter_context(nc.sbuf_tensor("cst_sb", [128, 640], BF16))
        wsl = [es.enter_context(nc.sbuf_tensor("wsl%d" % i, [128, 4096], BF16)) for i in range(2)]
        ARENA_B = 40 * 1024
        arena = es.enter_context(nc.sbuf_tensor("arena", [128, ARENA_B // 2], BF16))
        ps = es.enter_context(nc.psum_tensor("ps", [128, 8, 512], F32))
        arena32 = arena.bitcast(F32)
        ybuf32 = ybuf.bitcast(F32)

        yatt = ybuf[:, 0:4 * NT].rearrange("p (c t) -> p c t", c=4)
        yrnn = ybuf[:, 4 * NT:12 * NT].rearrange("p (c t) -> p c t", c=8)
        YR_B = 4 * NT * 2

        ident = cst[:, 0:128]
        ones = cst[:, 128:256]
        PT = cst[:, 256:384]
        maskb = cst[:, 384:640]

        p = Prog(nc)
        state = {'ps': 0, 'w': 0}

        def psn():
            b = state['ps']
            state['ps'] = (b + 1) % 8
            return b

        def mm(out, lhsT, rhs, start, stop, reads, writes):
            p.tag = p.tag.split(':')[0] + ':%d' % rhs.shape[-1]
            p.op('tensor', lambda e: e.matmul(out, lhsT=lhsT, rhs=rhs, start=start, stop=stop), reads, writes)

        def act(out, in_, func, reads, writes, scale=None, bias=None):
            kw = {}
            if scale is not None:
                kw['scale'] = scale
            if bias is not None:
                kw['bias'] = bias
            p.op('scalar', lambda e: e.activation(out=out, in_=in_, func=func, **kw), reads, writes)

        def vts(out, in0, s1, s2, op0, op1, reads, writes):
            if op1 is None:
                p.op('vector', lambda e: e.tensor_scalar(out=out, in0=in0, scalar1=s1, scalar2=None, op0=op0), reads, writes)
            else:
                p.op('vector', lambda e: e.tensor_scalar(out=out, in0=in0, scalar1=s1, scalar2=s2, op0=op0, op1=op1), reads, writes)

        def vtt(out, in0, in1, op, reads, writes):
            p.op('vector', lambda e: e.tensor_tensor(out=out, in0=in0, in1=in1, op=op), reads, writes)

        def vstt(out, in0, scalar, in1, op0, op1, reads, writes):
            p.op('vector', lambda e: e.scalar_tensor_tensor(out=out, in0=in0, scalar=scalar, in1=in1, op0=op0, op1=op1), reads, writes)

        def vcopy(out, in_, reads, writes):
            p.op('vector', lambda e: e.tensor_copy(out=out, in_=in_), reads, writes)

        plan = []
        for l_ in range(L):
            plan += [(watt_d[l_ * 12 + i], 3072) for i in range(12)]
            plan += [(wrnn_d[l_ * 8 + i], 2304) for i in range(8)]
            for hf in range(2):
                plan += [(wmrg_d[l_ * 8 + hf * 4 + i], 3584) for i in range(4)]
                plan += [(wout_d[l_ * 2 + hf], 4096)]
            for hf in range(2):
                plan += [(wup_d[l_ * 24 + hf * 12 + i], 2048) for i in range(12)]
                plan += [(wdn_d[(l_ * 2 + hf) * 8 + i], 1536) for i in range(8)]

        def wissue(k):
            src, n = plan[k]
            s = k % 2
            p.dma('gpsimd', 'w%d' % s, lambda e: e.dma_start(out=wsl[s][:, 0:n], in_=src), writes=[('w', s)])

        def wload(src, n):
            i = state['w']
            state['w'] += 1
            assert plan[i][1] == n, (i, plan[i][1], n)
            if i == 0:
                wissue(0)
            if i + 1 < len(plan):
                wissue(i + 1)
            return i % 2

        def gtt(out, in0, in1, op, reads, writes):
            p.op('gpsimd', lambda e: e.tensor_tensor(out=out, in0=in0, in1=in1, op=op), reads, writes)

        def gcopy(out, in_, reads, writes):
            p.op('gpsimd', lambda e: e.tensor_copy(out=out, in_=in_), reads, writes)

        def a16(off, n):
            return arena[:, off // 2: off // 2 + n]

        def a32(off, n):
            return arena32[:, off // 4: off // 4 + n]

        def y16(off, n):
            return ybuf[:, off // 2: off // 2 + n]

        def y32(off, n):
            return ybuf32[:, off // 4: off // 4 + n]

        HALL = [('h', i) for i in range(4)]
        tcs = [slice(i * 512, (i + 1) * 512) for i in range(4)]

        def hsl(tc):
            return slice(PADC + tc * 512, PADC + (tc + 1) * 512)

        p.dma('sync', 'vecs', lambda e: e.dma_start(out=vecs[:], in_=vec_d), writes=['vecs'])
        p.dma('gpsimd', 'cst', lambda e: e.dma_start(out=cst[:], in_=cst_d), writes=['cst'])
        for c in range(8):
            p.dma('sync', 'xin%d' % c, lambda e, c=c: e.dma_start(out=xT[:, c, :], in_=x_d[c]),
                  writes=[('x', c, t) for t in range(4)])
        p.op('vector', lambda e: e.memset(hpad[:, :, 0:PADC], 0.0), [], ['hpz'])
        posi = arena.bitcast(I32)[:, 0:NT]
        ang = a32(8192, NT)
        t_a = a32(16384, NT)
        t_b = a32(24576, NT)
        t_k = arena.bitcast(I32)[:, 8192:8192 + NT]
        pos_b = bass.AP(pos_d.tensor, 0, [[0, 128], [1, NT]])
        p.dma('sync', 'pos', lambda e: e.dma_start(out=posi, in_=pos_b), writes=['posi'])
        vcopy(ang, posi, ['posi'], ['ang'])
        invf = vecs[:, L * NV:L * NV + 1]
        vts(ang, ang, invf, None, ALU.mult, None, ['ang', 'vecs'], ['ang'])
        for which, shift in ((0, np.pi / 2), (1, 0.0)):
            vts(t_a, ang, 1.0 / TWO_PI, shift / TWO_PI + 0.5, ALU.mult, ALU.add, ['ang'], ['t_a'])
            vcopy(t_k, t_a, ['t_a'], ['t_k'])
            vcopy(t_a, t_k, ['t_k'], ['t_a'])
            vts(t_b, ang, float(shift), None, ALU.add, None, ['ang'], ['t_b'])
            vstt(t_b, t_a, -CW1, t_b, ALU.mult, ALU.add, ['t_a', 't_b'], ['t_b'])
            vstt(t_b, t_a, -CW2, t_b, ALU.mult, ALU.add, ['t_a', 't_b'], ['t_b'])
            vts(t_a, t_b, -np.pi, TWO_PI, ALU.is_lt, ALU.mult, ['t_b'], ['t_a'])
            vtt(t_b, t_b, t_a, ALU.add, ['t_a', 't_b'], ['t_b'])
            vts(t_a, t_b, np.pi, -TWO_PI, ALU.is_gt, ALU.mult, ['t_b'], ['t_a'])
            vtt(t_b, t_b, t_a, ALU.add, ['t_a', 't_b'], ['t_b'])
            vts(t_b, t_b, -3.1415925, 3.1415925, ALU.max, ALU.min, ['t_b'], ['t_b'])
            act(t_b, t_b, AF.Sin, ['t_b'], ['t_b'])
            p.dma('sync', 'cs%d' % which, lambda e, which=which: e.dma_start(out=cs_d[which], in_=t_b),
                  reads=['t_b'], writes=[('csd', which)])
        for l in range(L):
            vb, db = l * NV, l * NDV
            vts(dvec[:, db + D_HBA:db + D_HBA + 8], vecs[:, vb + V_BA:vb + V_BA + 8], 0.5, None, ALU.mult, None, ['vecs'], ['dvec'])
            vts(dvec[:, db + D_HBX:db + D_HBX + 8], vecs[:, vb + V_BX:vb + V_BX + 8], 0.5, None, ALU.mult, None, ['vecs'], ['dvec'])
            act(dvec[:, db + D_C8:db + D_C8 + 8], vecs[:, vb + V_LAM:vb + V_LAM + 8], AF.Exp, ['vecs'], ['dvec'], scale=-1.0)
            act(dvec[:, db + D_C8:db + D_C8 + 8], dvec[:, db + D_C8:db + D_C8 + 8], AF.Ln, ['dvec'], ['dvec'], bias=1.0)
            vts(dvec[:, db + D_C8H:db + D_C8H + 8], dvec[:, db + D_C8:db + D_C8 + 8], -4.0, None, ALU.mult, None, ['dvec'], ['dvec'])
            vts(dvec[:, db + D_C8:db + D_C8 + 8], dvec[:, db + D_C8:db + D_C8 + 8], -8.0, None, ALU.mult, None, ['dvec'], ['dvec'])
        p.barrier()

        def norm_phase(gcol):
            sq = [a16(0, 512), a16(1024, 512)]
            lnv = [a32(2048, 512), a32(4096, 512)]
            rstd = [a32(6144, 512), a32(8192, 512)]
            for tc in range(4):
                b = psn()
                for c in range(8):
                    act(sq[c % 2], xT[:, c, tcs[tc]], AF.Square, [('x', c, tc)], [('sq', c % 2)])
                    mm(ps[:, b, :], ones, sq[c % 2], c == 0, c == 7, [('sq', c % 2), 'cst'], [('ps', b)])
                act(lnv[tc % 2], ps[:, b, :], AF.Ln, [('ps', b)], [('lnv', tc % 2)], scale=1.0 / 1024, bias=EPS)
                act(rstd[tc % 2], lnv[tc % 2], AF.Exp, [('lnv', tc % 2)], [('rstd', tc % 2)], scale=-0.5)
                for c in range(8):
                    vstt(hpad[:, c, hsl(tc)], xT[:, c, tcs[tc]], vecs[:, gcol + c:gcol + c + 1], rstd[tc % 2],
                         ALU.mult, ALU.mult, [('x', c, tc), ('rstd', tc % 2), 'vecs'], [('h', tc)])

        def run_pipeline(units):
            n = len(units)
            maxs = max(len(u) for u in units)
            for t in range(n + maxs - 1):
                for k in range(maxs):
                    ui = t - k
                    if 0 <= ui < n and k < len(units[ui]):
                        units[ui][k]()

        def attn_phase(l):
            vb = l * NV
            Qb = [y16(YR_B + 0, NT), y16(YR_B + 4096, NT)]
            Kb = [y16(YR_B + 8192, NT), y16(YR_B + 12288, NT)]
            Vb = [y16(YR_B + 16384, NT).rearrange("p (t d) -> p t d", t=16),
                  y16(YR_B + 20480, NT).rearrange("p (t d) -> p t d", t=16)]
            accn = y32(YR_B + 24576, NT)
            accd = a32(0, NT)
            cosT = a32(8192, NT)
            sinT = a32(16384, NT)
            sq = [a16(24576, 512), a16(25600, 512)]
            qn = [a16(26624, 512), a16(27648, 512)]
            lnv = [a32(28672, 512), a32(30720, 512)]
            t1 = [a32(32768, 512), a32(34816, 512)]
            t2 = a32(36864, 512)
            pt = [a16(38912, 256), a16(39424, 256), a16(39936, 256), a16(40448, 256)]
            p.dma('sync', 'csl0', lambda e: e.dma_start(out=cosT, in_=cs_d[0]), reads=[('csd', 0)], writes=['cos'])
            p.dma('sync', 'csl1', lambda e: e.dma_start(out=sinT, in_=cs_d[1]), reads=[('csd', 1)], writes=['sin'])
            cnt = {'pu': 0, 'au': 0, 'pair': 0, 'nu': 0}
            heads = [(j, g) for j in range(4) for g in range(3)]

            def proj_units(hi):
                j, g = heads[hi]
                h = 4 * g + j
                d = DIL[g]
                M = NT // d
                nb = M // 128
                hb = hi % 2
                box = {}
                units = []

                def load():
                    s = wload(watt_d[l * 12 + hi], 3072)
                    box['s'] = s
                    box['ws'] = wsl[s][:, 0:3072].rearrange("p (k n) -> p k n", k=8)

                first = [True]
                for which, dst, gcol in ((0, Qb[hb], V_GQ), (1, Kb[hb], V_GK)):
                    dkey = ('Q' if which == 0 else 'K', hb)
                    for tc in range(4):
                        u = cnt['pu'] % 2
                        cnt['pu'] += 1
                        b = u
                        do_load = first[0]
                        first[0] = False

                        def s0(which=which, tc=tc, u=u, b=b, do_load=do_load):
                            p.tag = 'A.s0'
                            if do_load:
                                load()
                            s, ws = box['s'], box['ws']
                            for k in range(8):
                                mm(ps[:, b, :], ws[:, k, which * 128:(which + 1) * 128], hpad[:, k, hsl(tc)], k == 0, k == 7,
                                   [('w', s), ('h', tc)], [('ps', b)])
                            act(sq[u], ps[:, b, :], AF.Square, [('ps', b)], [('sq', u)])

                        def s1(u=u, b=b, gcol=gcol):
                            p.tag = 'A.s1'
                            mm(ps[:, 2, :], ones, sq[u], True, True, [('sq', u), 'cst'], [('ps', 2)])
                            act(lnv[u], ps[:, 2, :], AF.Ln, [('ps', 2)], [('lnv', u)], scale=1.0 / 128, bias=EPS)
                            act(lnv[u], lnv[u], AF.Exp, [('lnv', u)], [('lnv', u)], scale=-0.5)
                            vstt(qn[u], ps[:, b, :], vecs[:, vb + gcol + h:vb + gcol + h + 1], lnv[u], ALU.mult, ALU.mult,
                                 [('ps', b), ('lnv', u), 'vecs'], [('qn', u)])

                        def s2(u=u, tc=tc, dst=dst, dkey=dkey):
                            p.tag = 'A.s2'
                            mm(ps[:, 3, :], PT, qn[u], True, True, [('qn', u), 'cst'], [('ps', 3)])
                            gtt(t1[u], qn[u], cosT[:, tcs[tc]], ALU.mult, [('qn', u), 'cos'], [('t1', u)])
                            vtt(t2, ps[:, 3, :], sinT[:, tcs[tc]], ALU.mult, [('ps', 3), 'sin'], ['t2'])
                            if d == 1:
                                o_ap = dst[:, tcs[tc]]
                                i0, i1 = t1[u], t2
                            else:
                                m0 = tc * 512 // d
                                o_ap = dst.rearrange("p (r m) -> p m r", r=d)[:, m0:m0 + 512 // d, :]
                                i0 = t1[u].rearrange("p (m r) -> p m r", r=d)
                                i1 = t2.rearrange("p (m r) -> p m r", r=d)
                            vtt(o_ap, i0, i1, ALU.add, [('t1', u), 't2'], [dkey])

                        units.append([s0, s1, s2])
                for vg in range(4):
                    u = cnt['pu'] % 2
                    cnt['pu'] += 1

                    def v0(vg=vg, b=u):
                        p.tag = 'A.v0'
                        s, ws = box['s'], box['ws']
                        for ti in range(vg * 4, vg * 4 + 4):
                            r, bb = divmod(ti, nb)
                            st0 = PADC + r + d * bb * 128
                            for k in range(8):
                                mm(ps[:, b, (ti % 4) * 128:(ti % 4 + 1) * 128], hpad[:, k, st0:st0 + d * 127 + 1:d],
                                   ws[:, k, 256:384], k == 0, k == 7, [('w', s)] + HALL, [('ps', b)])
                        act(Vb[hb][:, vg * 4:vg * 4 + 4, :], ps[:, b, :].rearrange("p (t d) -> p t d", t=4), AF.Copy,
                            [('ps', b)], [('V', hb)])

                    units.append([v0])
                return units

            def attn_units(hi):
                j, g = heads[hi]
                d = DIL[g]
                M = NT // d
                nb = M // 128
                hb = hi % 2
                Q, K, V = Qb[hb], Kb[hb], Vb[hb]
                rq = [('Q', hb), ('K', hb)]
                units = []
                for ti in range(16):
                    r, bb = divmod(ti, nb)
                    qc = slice(ti * 128, ti * 128 + 128)
                    pc = slice((ti - 1) * 128, ti * 128)
                    au = cnt['au']
                    cnt['au'] += 1
                    bs = 4 + au % 2
                    pi_ = au % 4
                    if ti % 2 == 0:
                        cnt['pair'] += 1
                    bo = 6 + cnt['pair'] % 2
                    half = ti % 2
                    oc = slice(half * 128, half * 128 + 128)
                    dc = slice(256 + half * 128, 256 + half * 128 + 128)

                    def t0(bb=bb, qc=qc, pc=pc, bs=bs, pi_=pi_):
                        p.tag = 'A.t0'
                        if bb > 0:
                            mm(ps[:, bs, 0:256], ident, maskb[:, 0:256], True, False, ['cst'], [('ps', bs)])
                            mm(ps[:, bs, 0:128], K[:, pc], Q[:, qc], False, False, rq, [('ps', bs)])
                            mm(ps[:, bs, 128:256], K[:, qc], Q[:, qc], False, True, rq, [('ps', bs)])
                            act(pt[pi_], ps[:, bs, 0:256], AF.Exp, [('ps', bs)], [('pt', pi_)], scale=SC)
                        else:
                            mm(ps[:, bs, 0:128], ident, maskb[:, 128:256], True, False, ['cst'], [('ps', bs)])
                            mm(ps[:, bs, 0:128], K[:, qc], Q[:, qc], False, True, rq, [('ps', bs)])
                            act(pt[pi_][:, 0:128], ps[:, bs, 0:128], AF.Exp, [('ps', bs)], [('pt', pi_)], scale=SC)

                    def t1_(ti=ti, bb=bb, bo=bo, oc=oc, dc=dc, pi_=pi_, half=half):
                        p.tag = 'A.t1'
                        rv = [('V', hb), ('pt', pi_)]
                        rc = ['cst', ('pt', pi_)]
                        if bb > 0:
                            mm(ps[:, bo, oc], V[:, ti - 1, :], pt[pi_][:, 0:128], True, False, rv, [('ps', bo)])
                            mm(ps[:, bo, oc], V[:, ti, :], pt[pi_][:, 128:256], False, True, rv, [('ps', bo)])
                            mm(ps[:, bo, dc], ones, pt[pi_][:, 0:128], True, False, rc, [('ps', bo)])
                            mm(ps[:, bo, dc], ones, pt[pi_][:, 128:256], False, True, rc, [('ps', bo)])
                        else:
                            mm(ps[:, bo, oc], V[:, ti, :], pt[pi_][:, 0:128], True, True, rv, [('ps', bo)])
                            mm(ps[:, bo, dc], ones, pt[pi_][:, 0:128], True, True, rc, [('ps', bo)])
                        if half == 1:
                            e2 = ti // 2
                            for c0, acc, akey in ((0, accn, 'accn'), (256, accd, 'accd')):
                                src = ps[:, bo, c0:c0 + 256]
                                if g == 0:
                                    dstv = acc[:, e2 * 256:(e2 + 1) * 256]
                                elif g == 1:
                                    rr, hh2 = divmod(e2, 2)
                                    dstv = acc[:, rr + 4 * 256 * hh2:rr + 4 * 256 * hh2 + 4 * 255 + 1:4]
                                else:
                                    dstv = acc.rearrange("p (i r) -> p r i", r=16)[:, 2 * e2:2 * e2 + 2, :]
                                    src = src.rearrange("p (q i) -> p q i", q=2)
                                if g == 0:
                                    vcopy(dstv, src, [('ps', bo)], [akey])
                                else:
                                    vtt(dstv, src, dstv, ALU.add, [('ps', bo), akey], [akey])

                    units.append([t0, t1_])
                if g == 2:
                    for tc in range(4):
                        u = cnt['nu'] % 2
                        cnt['nu'] += 1

                        def n0(tc=tc, u=u):
                            pass

                        def n1(tc=tc, u=u):
                            pass

                        def n2(tc=tc, u=u):
                            p.tag = 'A.n2'
                            act(t1[u], accd[:, tcs[tc]], AF.Ln, ['accd'], [('t1', u)])
                            act(t1[u], t1[u], AF.Exp, [('t1', u)], [('t1', u)], scale=-1.0)
                            vtt(yatt[:, j, tcs[tc]], accn[:, tcs[tc]], t1[u], ALU.mult, ['accn', ('t1', u)], [('yatt', j)])

                        units.append([n0, n1, n2])
                return units

            def interleave(a, b):
                out = []
                na, nb_ = len(a), len(b)
                ia = ib = 0
                while ia < na or ib < nb_:
                    if ib >= nb_ or (ia < na and (ia + 1) * nb_ <= (ib + 1) * na):
                        out.append(a[ia])
                        ia += 1
                    else:
                        out.append(b[ib])
                        ib += 1
                return out

            def steps_of(units, order=(0, 1, 2)):
                n = len(units)
                maxs = max(len(u) for u in units)
                steps = []
                for t in range(n + maxs - 1):
                    st = []
                    for k in order:
                        ui = t - k
                        if 0 <= ui < n and k < len(units[ui]):
                            st.append(units[ui][k])
                    steps.append(st)
                return steps

            for hs in range(13):
                psteps = steps_of(proj_units(hs), (1, 0, 2)) if hs < 12 else []
                asteps = steps_of(attn_units(hs - 1)) if hs >= 1 else []
                for st in interleave(psteps, asteps):
                    for fn in st:
                        fn()

        def rnn_phase(l):
            vb, db = l * NV, l * NDV
            xc = a32(0, NT)
            thr = a32(8192, NT)
            gg = a16(16384, NT)
            xcb = [a16(20480, 512), a16(21504, 512)]
            av = [a32(22528, 512), a32(24576, 512)]
            a2 = [a32(26624, 512), a32(28672, 512)]
            uu = [a32(30720, 512), a32(32768, 512)]
            hh = [a32(34816, 512), a32(36864, 512)]
            cnt = 0
            for c in range(8):
                s = wload(wrnn_d[l * 8 + c], 2304)
                wxg = wsl[s][:, 0:2048].rearrange("p (k n) -> p k n", k=8)
                wa = wsl[s][:, 2048:2176]
                wx = wsl[s][:, 2176:2304]
                for w in range(5):
                    c0 = 509 * w
                    wl = min(512, PADC + NT - c0)
                    nv = wl - 3
                    b = psn()
                    for k in range(8):
                        mm(ps[:, b, 0:wl], wxg[:, k, 0:128], hpad[:, k, c0:c0 + wl], k == 0, k == 7,
                           [('w', s), 'hpz'] + HALL, [('ps', b)])
                    act(xc[:, c0:c0 + nv], ps[:, b, 3:3 + nv], AF.Identity, [('ps', b), 'vecs'], ['xc'],
                        scale=vecs[:, vb + V_CW + 24 + c:vb + V_CW + 25 + c], bias=vecs[:, vb + V_CB + c:vb + V_CB + c + 1])
                    for i in range(3):
                        vstt(xc[:, c0:c0 + nv], ps[:, b, i:i + nv], vecs[:, vb + V_CW + 8 * i + c:vb + V_CW + 8 * i + c + 1],
                             xc[:, c0:c0 + nv], ALU.mult, ALU.add, [('ps', b), 'xc', 'vecs'], ['xc'])
                for tc in range(4):
                    u = tc % 2
                    vcopy(xcb[u], xc[:, tcs[tc]], ['xc'], [('xcb', u)])
                    b1 = psn()
                    mm(ps[:, b1, :], wa, xcb[u], True, True, [('w', s), ('xcb', u)], [('ps', b1)])
                    b2 = psn()
                    mm(ps[:, b2, :], wx, xcb[u], True, True, [('w', s), ('xcb', u)], [('ps', b2)])
                    act(thr[:, tcs[tc]], ps[:, b1, :], AF.Tanh, [('ps', b1), 'dvec'], [('thr', tc)], scale=0.5,
                        bias=dvec[:, db + D_HBA + c:db + D_HBA + c + 1])
                    act(uu[u], ps[:, b2, :], AF.Tanh, [('ps', b2), 'dvec'], [('uu', u)], scale=0.5,
                        bias=dvec[:, db + D_HBX + c:db + D_HBX + c + 1])
                    vstt(xc[:, tcs[tc]], uu[u], 1.0, xc[:, tcs[tc]], ALU.add, ALU.mult, [('uu', u), 'xc', ('xcb', u)], ['xc'])
                    b3 = psn()
                    for k in range(8):
                        mm(ps[:, b3, :], wxg[:, k, 128:256], hpad[:, k, hsl(tc)], k == 0, k == 7,
                           [('w', s), ('h', tc)], [('ps', b3)])
                    act(gg[:, tcs[tc]], ps[:, b3, :], AF.Gelu_apprx_tanh, [('ps', b3)], [('gg', tc)])
                for tc in range(4):
                    u = cnt % 2
                    cnt += 1
                    c8 = dvec[:, db + D_C8 + c:db + D_C8 + c + 1]
                    c8h = dvec[:, db + D_C8H + c:db + D_C8H + c + 1]
                    act(av[u], thr[:, tcs[tc]], AF.Exp, [('thr', tc), 'dvec'], [('av', u)], scale=c8h, bias=c8h)
                    act(a2[u], thr[:, tcs[tc]], AF.Exp, [('thr', tc), 'dvec'], [('a2', u)], scale=c8, bias=c8)
                    act(a2[u], a2[u], AF.Ln, [('a2', u)], [('a2', u)], scale=-1.0, bias=1.0)
                    act(a2[u], a2[u], AF.Exp, [('a2', u)], [('a2', u)], scale=0.5)
                    vstt(uu[u], xc[:, tcs[tc]], 0.5, a2[u], ALU.mult, ALU.mult, ['xc', ('a2', u)], [('uu', u)])
                    init = 0.0 if tc == 0 else hh[1 - u][:, 511:512]
                    rd = [('av', u), ('uu', u)] + ([] if tc == 0 else [('hh', 1 - u)])
                    p.op('vector', lambda e, u=u, init=init: e.tensor_tensor_scan(
                        out=hh[u], data0=av[u], data1=uu[u], initial=init, op0=ALU.mult, op1=ALU.add), rd, [('hh', u)])
                    gtt(yrnn[:, c, tcs[tc]], hh[u], gg[:, tcs[tc]], ALU.mult, [('hh', u), ('gg', tc)], [('yrnn', c)])

        def merge_phase(l):
            mg = a16(0, 4 * NT).rearrange("p (c t) -> p c t", c=4)
            th1 = [a32(16384, 512), a32(18432, 512)]
            th2 = [a32(20480, 512), a32(22528, 512)]
            m1 = [a32(24576, 512), a32(26624, 512)]
            cnt = 0
            YALL = [('yrnn', c) for c in range(8)]
            AALL = [('yatt', j) for j in range(4)]
            for half in range(2):
                for ci in range(4):
                    c = half * 4 + ci
                    s = wload(wmrg_d[l * 8 + c], 3584)
                    prn = wsl[s][:, 0:1024].rearrange("p (k n) -> p k n", k=8)
                    pat = wsl[s][:, 1024:1536].rearrange("p (k n) -> p k n", k=4)
                    wg = wsl[s][:, 1536:3584].rearrange("p (k n) -> p k n", k=8)
                    for tc in range(4):
                        u = cnt % 2
                        cnt += 1
                        bA, bB, bG1, bG2 = psn(), psn(), psn(), psn()
                        for k in range(8):
                            mm(ps[:, bG1, :], wg[:, k, 0:128], hpad[:, k, hsl(tc)], k == 0, k == 7, [('w', s), ('h', tc)], [('ps', bG1)])
                        for k in range(8):
                            mm(ps[:, bG2, :], wg[:, k, 128:256], hpad[:, k, hsl(tc)], k == 0, k == 7, [('w', s), ('h', tc)], [('ps', bG2)])
                        for k in range(8):
                            mm(ps[:, bA, :], prn[:, k, :], yrnn[:, k, tcs[tc]], k == 0, k == 7, [('w', s)] + YALL, [('ps', bA)])
                        for k in range(4):
                            mm(ps[:, bB, :], pat[:, k, :], yatt[:, k, tcs[tc]], k == 0, k == 3, [('w', s)] + AALL, [('ps', bB)])
                        act(th1[u], ps[:, bG1, :], AF.Tanh, [('ps', bG1)], [('th1', u)], scale=0.5)
                        act(th2[u], ps[:, bG2, :], AF.Tanh, [('ps', bG2)], [('th2', u)], scale=0.5)
                        vstt(m1[u], th1[u], 1.0, ps[:, bA, :], ALU.add, ALU.mult, [('th1', u), ('ps', bA)], [('m1', u)])
                        vstt(th2[u], th2[u], 1.0, ps[:, bB, :], ALU.add, ALU.mult, [('th2', u), ('ps', bB)], [('th2', u)])
                        vtt(mg[:, ci, tcs[tc]], m1[u], th2[u], ALU.add, [('m1', u), ('th2', u)], [('mg', ci)])
                s = wload(wout_d[l * 2 + half], 4096)
                wo = wsl[s][:, 0:4096].rearrange("p (c k n) -> p c k n", c=8, k=4)
                for c2 in range(8):
                    for tc in range(4):
                        b = psn()
                        for ci in range(4):
                            mm(ps[:, b, :], wo[:, c2, ci, :], mg[:, ci, tcs[tc]], ci == 0, ci == 3,
                               [('w', s)] + [('mg', i) for i in range(4)], [('ps', b)])
                        vstt(xT[:, c2, tcs[tc]], ps[:, b, :], 0.5, xT[:, c2, tcs[tc]], ALU.mult, ALU.add,
                             [('ps', b), ('x', c2, tc)], [('x', c2, tc)])

        def ffn_phase(l):
            vb = l * NV
            acth = ybuf[:, 0:12 * NT].rearrange("p (f t) -> p f t", f=12)
            ug = [a32(0, 512), a32(2048, 512)]
            uv = [a32(4096, 512), a32(6144, 512)]
            cnt = 0
            for half in range(2):
                for fi in range(12):
                    f = half * 12 + fi
                    s = wload(wup_d[l * 24 + f], 2048)
                    wu = wsl[s][:, 0:2048].rearrange("p (k n) -> p k n", k=8)
                    for w in range(5):
                        c0 = 1 + 510 * w
                        wl = min(512, PADC + NT - c0)
                        nv = wl - 2
                        u = cnt % 2
                        cnt += 1
                        bg, bv = psn(), psn()
                        for k in range(8):
                            mm(ps[:, bg, 0:wl], wu[:, k, 0:128], hpad[:, k, c0:c0 + wl], k == 0, k == 7,
                               [('w', s), 'hpz'] + HALL, [('ps', bg)])
                        for k in range(8):
                            mm(ps[:, bv, 0:wl], wu[:, k, 128:256], hpad[:, k, c0:c0 + wl], k == 0, k == 7,
                               [('w', s), 'hpz'] + HALL, [('ps', bv)])
                        for bank, dst, key, fc in ((bg, ug[u], ('ug', u), f), (bv, uv[u], ('uv', u), 24 + f)):
                            def wcol(i, fc=fc):
                                return vecs[:, vb + V_FW + 48 * i + fc:vb + V_FW + 48 * i + fc + 1]
                            act(dst[:, 0:nv], ps[:, bank, 2:2 + nv], AF.Identity, [('ps', bank), 'vecs'], [key],
                                scale=wcol(2), bias=vecs[:, vb + V_FB + fc:vb + V_FB + fc + 1])
                            for i in range(2):
                                vstt(dst[:, 0:nv], ps[:, bank, i:i + nv], wcol(i), dst[:, 0:nv], ALU.mult, ALU.add,
                                     [('ps', bank), key, 'vecs'], [key])
                        act(ug[u][:, 0:nv], ug[u][:, 0:nv], AF.Gelu_apprx_tanh, [('ug', u)], [('ug', u)])
                        gtt(acth[:, fi, 510 * w:510 * w + nv], ug[u][:, 0:nv], uv[u][:, 0:nv], ALU.mult,
                            [('ug', u), ('uv', u)], [('act', fi)])
                for c in range(8):
                    s = wload(wdn_d[(l * 2 + half) * 8 + c], 1536)
                    wd = wsl[s][:, 0:1536].rearrange("p (k n) -> p k n", k=12)
                    for tc in range(4):
                        b = psn()
                        for fi in range(12):
                            mm(ps[:, b, :], wd[:, fi, :], acth[:, fi, tcs[tc]], fi == 0, fi == 11,
                               [('w', s)] + [('act', i) for i in range(12)], [('ps', b)])
                        vtt(xT[:, c, tcs[tc]], ps[:, b, :], xT[:, c, tcs[tc]], ALU.add, [('ps', b), ('x', c, tc)], [('x', c, tc)])

        for l in range(L):
            p.tag = 'N1'
            norm_phase(l * NV + V_G1)
            p.barrier()
            attn_phase(l)
            p.barrier()
            p.tag = 'R'
            rnn_phase(l)
            p.barrier()
            p.tag = 'M'
            merge_phase(l)
            p.barrier()
            p.tag = 'N2'
            norm_phase(l * NV + V_G2)
            p.barrier()
            p.tag = 'F'
            ffn_phase(l)
            p.barrier()
        for c in range(8):
            p.dma('sync', 'yout%d' % c, lambda e, c=c: e.dma_start(out=y_d[c], in_=xT[:, c, :]),
                  reads=[('x', c, t) for t in range(4)])
        p.finish('sync')
        p.emit(es)
        import os as _os
        if _os.environ.get('MK_TAGS'):
            import json as _json
            _json.dump(p.tags, open(_os.environ['MK_TAGS'], 'w'))
    return nc


def _chunk_k(W):
    K, N = W.shape
    return W.reshape(K // 128, 128, N).transpose(1, 0, 2)


def _colvec(v, n):
    return np.asarray(v).reshape(n, 128).T


def _consts():
    c = np.zeros((128, 640), np.float32)
    c[:, 0:128] = np.eye(128, dtype=np.float32)
    c[:, 128:256] = 1.0
    for m in range(128):
        if m < 64:
            c[m + 64, 256 + m] = -1.0
        else:
            c[m - 64, 256 + m] = 1.0
    i = np.arange(128)[:, None]
    j = np.arange(128)[None, :]
    c[:, 384:512] = np.where(i >= j, 0.0, NEG)
    c[:, 512:640] = np.where(i <= j, 0.0, NEG)
    return c


def _prep_weights(L, w_in, lru_wa, lru_wx, proj_rnn, proj_attn, w_out, w_up, w_down):
    watt = np.empty((L * 12, 128, 3072), np.float32)
    wrnn = np.empty((L * 8, 128, 2304), np.float32)
    wmrg = np.empty((L * 8, 128, 3584), np.float32)
    wout = np.empty((L * 2, 128, 4096), np.float32)
    wup = np.empty((L * 24, 128, 2048), np.float32)
    wdn = np.empty((L * 16, 128, 1536), np.float32)
    for l in range(L):
        wi = w_in[l]
        for j in range(4):
            for g in range(3):
                h = 4 * g + j
                cols = np.concatenate([wi[:, 2048 + h * 128:2048 + (h + 1) * 128],
                                       wi[:, 3584 + h * 128:3584 + (h + 1) * 128],
                                       wi[:, 5120 + h * 128:5120 + (h + 1) * 128]], axis=1)
                watt[l * 12 + j * 3 + g] = _chunk_k(cols).reshape(128, 3072)
        for c in range(8):
            cols = np.concatenate([wi[:, c * 128:(c + 1) * 128], wi[:, 1024 + c * 128:1024 + (c + 1) * 128]], axis=1)
            wrnn[l * 8 + c, :, 0:2048] = _chunk_k(cols).reshape(128, 2048)
            wrnn[l * 8 + c, :, 2048:2176] = lru_wa[l, c]
            wrnn[l * 8 + c, :, 2176:2304] = lru_wx[l, c]
            wmrg[l * 8 + c, :, 0:1024] = _chunk_k(proj_rnn[l][:, c * 128:(c + 1) * 128]).reshape(128, 1024)
            wmrg[l * 8 + c, :, 1024:1536] = _chunk_k(proj_attn[l][:, c * 128:(c + 1) * 128]).reshape(128, 512)
            gcols = np.concatenate([wi[:, 6656 + c * 128:6656 + (c + 1) * 128], wi[:, 7680 + c * 128:7680 + (c + 1) * 128]], axis=1)
            wmrg[l * 8 + c, :, 1536:3584] = _chunk_k(gcols).reshape(128, 2048)
        for half in range(2):
            blk = w_out[l][half * 512:(half + 1) * 512, :]
            a = blk.reshape(4, 128, 8, 128).transpose(1, 2, 0, 3)
            wout[l * 2 + half] = a.reshape(128, 4096)
        for f in range(24):
            cols = np.concatenate([w_up[l][:, f * 128:(f + 1) * 128], w_up[l][:, 3072 + f * 128:3072 + (f + 1) * 128]], axis=1)
            wup[l * 24 + f] = _chunk_k(cols).reshape(128, 2048)
        for half in range(2):
            for c in range(8):
                blk = w_down[l][half * 1536:(half + 1) * 1536, c * 128:(c + 1) * 128]
                wdn[(l * 2 + half) * 8 + c] = blk.reshape(12, 128, 128).transpose(1, 0, 2).reshape(128, 1536)
    return watt, wrnn, wmrg, wout, wup, wdn


def _prep_vecs(L, ln1_g, ln2_g, rnn_conv_w, rnn_conv_b, lru_ba, lru_bx, lru_lambda, q_norm_g, k_norm_g, ffn_conv_w, ffn_conv_b):
    out = np.zeros((128, L * NV + 1), np.float32)
    for l in range(L):
        parts = [_colvec(ln1_g[l], 8), _colvec(ln2_g[l], 8)]
        parts += [_colvec(rnn_conv_w[l][i], 8) for i in range(4)]
        parts += [_colvec(rnn_conv_b[l], 8), _colvec(lru_ba[l], 8), _colvec(lru_bx[l], 8), _colvec(lru_lambda[l], 8)]
        parts += [np.asarray(q_norm_g[l]).T, np.asarray(k_norm_g[l]).T]
        parts += [_colvec(ffn_conv_w[l][i], 48) for i in range(3)]
        parts += [_colvec(ffn_conv_b[l], 48)]
        out[:, l * NV:(l + 1) * NV] = np.concatenate(parts, axis=1)
    half = 64
    inv_freq = (np.float32(10000.0) ** (-np.arange(half, dtype=np.float32) / np.float32(half))).astype(np.float32)
    out[:, L * NV] = np.concatenate([inv_freq, inv_freq])
    return out


_NC_CACHE = {}


def _get_nc(L):
    if L not in _NC_CACHE:
        _NC_CACHE[L] = build(L)
    return _NC_CACHE[L]


def run_layers(x, positions, L, layer0, params, core_ids=None):
    B = x.shape[0]
    sl = slice(layer0, layer0 + L)
    f = lambda a: np.ascontiguousarray(np.asarray(a, dtype=np.float32)[sl])
    watt, wrnn, wmrg, wout, wup, wdn = _prep_weights(L, f(params['w_in']), f(params['lru_wa']), f(params['lru_wx']),
                                                    f(params['proj_rnn']), f(params['proj_attn']), f(params['w_out']),
                                                    f(params['w_up']), f(params['w_down']))
    vecs = _prep_vecs(L, f(params['ln1_g']), f(params['ln2_g']), f(params['rnn_conv_w']), f(params['rnn_conv_b']),
                      f(params['lru_ba']), f(params['lru_bx']), f(params['lru_lambda']), f(params['q_norm_g']),
                      f(params['k_norm_g']), f(params['ffn_conv_w']), f(params['ffn_conv_b']))
    cst = _consts()
    nc = _get_nc(L)
    in_maps = []
    for b in range(B):
        xt = np.ascontiguousarray(np.asarray(x[b], np.float32).T).reshape(8, 128, NT)
        in_maps.append({"x": xt, "pos": np.ascontiguousarray(np.asarray(positions[b], np.int32).reshape(1, NT)),
                        "vecs": vecs, "cst": cst, "w_att": watt, "w_rnn": wrnn, "w_mrg": wmrg, "w_out": wout,
                        "w_up": wup, "w_dn": wdn})
    res = run_bass_kernel_spmd(nc, in_maps, core_ids=list(range(B)) if core_ids is None else core_ids)
    out = np.empty((B, NT, 1024), np.float32)
    for b in range(B):
        out[b] = np.asarray(res.results[b]["y"], np.float32).reshape(1024, NT).T
    return out


def kernel(x, positions, ln1_g, w_in, rnn_conv_w, rnn_conv_b, lru_wa, lru_ba, lru_wx, lru_bx,
           lru_lambda, q_norm_g, k_norm_g, proj_rnn, proj_attn, w_out, ln2_g, w_up,
           ffn_conv_w, ffn_conv_b, w_down):
    params = dict(ln1_g=ln1_g, w_in=w_in, rnn_conv_w=rnn_conv_w, rnn_conv_b=rnn_conv_b, lru_wa=lru_wa, lru_ba=lru_ba,
                  lru_wx=lru_wx, lru_bx=lru_bx, lru_lambda=lru_lambda, q_norm_g=q_norm_g, k_norm_g=k_norm_g,
                  proj_rnn=proj_rnn, proj_attn=proj_attn, w_out=w_out, ln2_g=ln2_g, w_up=w_up,
                  ffn_conv_w=ffn_conv_w, ffn_conv_b=ffn_conv_b, w_down=w_down)
    return run_layers(np.asarray(x), np.asarray(positions), DEPTH, 0, params)
```

```python
from contextlib import ExitStack
import numpy as np
import concourse.bass as bass
import concourse.mybir as mybir
from concourse.bass_utils import run_bass_kernel_spmd

F32 = mybir.dt.float32
BF16 = mybir.dt.bfloat16
I32 = mybir.dt.int32
AF = mybir.ActivationFunctionType
ALU = mybir.AluOpType

DEPTH = 4
NT = 2048
PADC = 3
HW = 2052
EPS = 1e-6
NEG = -30000.0
SC = 128 ** -0.5
DIL = (1, 4, 16)
TWO_PI = 6.283185307179586
CW1 = 6.28125
CW2 = TWO_PI - CW1

V_G1, V_G2, V_CW, V_CB, V_BA, V_BX, V_LAM, V_GQ, V_GK, V_FW, V_FB, NV = 0, 8, 16, 48, 56, 64, 72, 80, 92, 104, 248, 296
D_HBA, D_HBX, D_C8, D_C8H, NDV = 0, 8, 16, 24, 32

ENGS = ['tensor', 'vector', 'scalar', 'gpsimd', 'sync']
EPOCH = 8192


class _Stream:
    def __init__(self, name):
        self.name = name
        self.items = []
        self.count = 0
        self.waited = {}
        self.last_tok = None


class Prog:
    def __init__(self, nc):
        self.nc = nc
        self.st = {e: _Stream(e) for e in ENGS}
        self.last_write = {}
        self.readers = {}
        self.slots = {}
        self.slot_eng = {}
        self.tag = ''
        self.tags = {e: [] for e in ENGS}

    def _need(self, eng, reads, writes):
        need = {}

        def add(tok):
            if tok is None:
                return
            k, v, e = tok
            if e == eng and eng == 'tensor':
                return
            if need.get(k, 0) < v:
                need[k] = v

        for r in reads:
            add(self.last_write.get(r))
        for w in writes:
            add(self.last_write.get(w))
            for k, (v, e) in self.readers.get(w, {}).items():
                add((k, v, e))
        return need

    def _commit_waits(self, s, need):
        waits = [(k, v) for k, v in need.items() if s.waited.get(k, 0) < v]
        for k, v in waits:
            s.waited[k] = v
        return waits

    def _record(self, tok, reads, writes):
        k, v, e = tok
        for w in writes:
            self.last_write[w] = tok
            self.readers[w] = {}
        for r in reads:
            d = self.readers.setdefault(r, {})
            if d.get(k, (0, None))[0] < v:
                d[k] = (v, e)

    def op(self, eng, fn, reads=(), writes=()):
        s = self.st[eng]
        need = self._need(eng, reads, writes)
        waits = self._commit_waits(s, need)
        s.count += 1
        ep, val = divmod(s.count - 1, EPOCH)
        tok = ('E:%s:%d' % (eng, ep), val + 1, eng)
        s.last_tok = tok
        s.items.append(('op', waits, fn, tok[0]))
        self.tags[eng].append(self.tag)
        self._record(tok, reads, writes)
        return tok

    def dma(self, eng, slot, fn, reads=(), writes=()):
        s = self.st[eng]
        need = self._need(eng, reads, writes)
        prev = self.slots.get(slot, 0)
        if prev:
            k = 'D:' + slot
            if need.get(k, 0) < prev:
                need[k] = prev
        waits = self._commit_waits(s, need)
        total = prev + 16
        self.slots[slot] = total
        self.slot_eng[slot] = eng
        tok = ('D:' + slot, total, None)
        s.items.append(('dma', waits, fn, slot))
        self._record(tok, reads, writes)
        return tok

    def barrier(self, engs=('tensor', 'vector', 'scalar', 'gpsimd', 'sync')):
        need = {}
        for e in engs:
            if self.st[e].count:
                k, v, _ = self.st[e].last_tok
                need[k] = v
        for slot, tot in self.slots.items():
            if self.slot_eng[slot] == 'sync':
                need['D:' + slot] = tot
        for e in engs:
            s = self.st[e]
            waits = self._commit_waits(s, dict(need))
            if waits:
                s.items.append(('wait', waits, None, None))

    def finish(self, eng='sync'):
        s = self.st[eng]
        need = {'D:' + slot: tot for slot, tot in self.slots.items()}
        for e in ENGS:
            if self.st[e].count:
                k, v, _ = self.st[e].last_tok
                need[k] = v
        waits = self._commit_waits(s, need)
        s.items.append(('wait', waits, None, None))

    def emit(self, es):
        nc = self.nc
        sems = {}
        for e in ENGS:
            for ep in range((self.st[e].count + EPOCH - 1) // EPOCH):
                sems['E:%s:%d' % (e, ep)] = es.enter_context(nc.semaphore('s_%s_%d' % (e, ep)))
        for slot in self.slots:
            sems['D:' + slot] = es.enter_context(nc.semaphore('d_' + slot))
        with nc.Block() as block:
            for e in ENGS:
                items = self.st[e].items
                if not items:
                    continue

                def body(engh, items=items, e=e):
                    for kind, waits, fn, slot in items:
                        for k, v in waits:
                            engh.wait_ge(sems[k], v)
                        if kind == 'op':
                            fn(engh).then_inc(sems[slot], 1)
                        elif kind == 'dma':
                            fn(engh).then_inc(sems['D:' + slot], 16)

                getattr(block, e)(body)


def build(L, dbg=None):
    nc = bass.Bass("TRN2", target_bir_lowering=False)
    x_d = nc.dram_tensor("x", [8, 128, NT], F32, kind="ExternalInput").ap()
    pos_d = nc.dram_tensor("pos", [1, NT], I32, kind="ExternalInput").ap()
    vec_d = nc.dram_tensor("vecs", [128, L * NV + 1], F32, kind="ExternalInput").ap()
    cst_d = nc.dram_tensor("cst", [128, 640], F32, kind="ExternalInput").ap()
    watt_d = nc.dram_tensor("w_att", [L * 12, 128, 3072], F32, kind="ExternalInput").ap()
    wrnn_d = nc.dram_tensor("w_rnn", [L * 8, 128, 2304], F32, kind="ExternalInput").ap()
    wmrg_d = nc.dram_tensor("w_mrg", [L * 8, 128, 3584], F32, kind="ExternalInput").ap()
    wout_d = nc.dram_tensor("w_out", [L * 2, 128, 4096], F32, kind="ExternalInput").ap()
    wup_d = nc.dram_tensor("w_up", [L * 24, 128, 2048], F32, kind="ExternalInput").ap()
    wdn_d = nc.dram_tensor("w_dn", [L * 16, 128, 1536], F32, kind="ExternalInput").ap()
    y_d = nc.dram_tensor("y", [8, 128, NT], F32, kind="ExternalOutput").ap()
    cs_d = nc.dram_tensor("cs_scr", [2, 128, NT], F32, kind="Internal").ap()

    with ExitStack() as es:
        xT = es.enter_context(nc.sbuf_tensor("xT", [128, 8, NT], F32))
        hpad = es.enter_context(nc.sbuf_tensor("hpad", [128, 8, HW], BF16))
        ybuf = es.enter_context(nc.sbuf_tensor("ybuf", [128, 12 * NT], BF16))
        vecs = es.enter_context(nc.sbuf_tensor("vecs_sb", [128, L * NV + 1], F32))
        dvec = es.enter_context(nc.sbuf_tensor("dvec", [128, L * NDV], F32))
        hst = es.enter_context(nc.sbuf_tensor("hst", [128, 8], F32))
        cst = es.enter_context(nc.sbuf_tensor("cst_sb", [128, 640], BF16))
        wsl = [es.enter_context(nc.sbuf_tensor("wsl%d" % i, [128, 4096], BF16)) for i in range(2)]
        ARENA_B = 40 * 1024
        arena = es.enter_context(nc.sbuf_tensor("arena", [128, ARENA_B // 2], BF16))
        ps = es.enter_context(nc.psum_tensor("ps", [128, 8, 512], F32))
        arena32 = arena.bitcast(F32)
        ybuf32 = ybuf.bitcast(F32)

        yatt = ybuf[:, 0:4 * NT].rearrange("p (c t) -> p c t", c=4)
        yrnn = ybuf[:, 4 * NT:12 * NT].rearrange("p (c t) -> p c t", c=8)
        YR_B = 4 * NT * 2

        ident = cst[:, 0:128]
        ones = cst[:, 128:256]
        PT = cst[:, 256:384]
        maskb = cst[:, 384:640]

        p = Prog(nc)
        state = {'ps': 0, 'w': 0}

        def psn():
            b = state['ps']
            state['ps'] = (b + 1) % 8
            return b

        def mm(out, lhsT, rhs, start, stop, reads, writes):
            p.tag = p.tag.split(':')[0] + ':%d' % rhs.shape[-1]
            p.op('tensor', lambda e: e.matmul(out, lhsT=lhsT, rhs=rhs, start=start, stop=stop), reads, writes)

        def act(out, in_, func, reads, writes, scale=None, bias=None):
            kw = {}
            if scale is not None:
                kw['scale'] = scale
            if bias is not None:
                kw['bias'] = bias
            p.op('scalar', lambda e: e.activation(out=out, in_=in_, func=func, **kw), reads, writes)

        def vts(out, in0, s1, s2, op0, op1, reads, writes):
            if op1 is None:
                p.op('vector', lambda e: e.tensor_scalar(out=out, in0=in0, scalar1=s1, scalar2=None, op0=op0), reads, writes)
            else:
                p.op('vector', lambda e: e.tensor_scalar(out=out, in0=in0, scalar1=s1, scalar2=s2, op0=op0, op1=op1), reads, writes)

        def vtt(out, in0, in1, op, reads, writes):
            p.op('vector', lambda e: e.tensor_tensor(out=out, in0=in0, in1=in1, op=op), reads, writes)

        def vstt(out, in0, scalar, in1, op0, op1, reads, writes):
            p.op('vector', lambda e: e.scalar_tensor_tensor(out=out, in0=in0, scalar=scalar, in1=in1, op0=op0, op1=op1), reads, writes)

        def vcopy(out, in_, reads, writes):
            p.op('vector', lambda e: e.tensor_copy(out=out, in_=in_), reads, writes)

        plan = []
        for l_ in range(L):
            plan += [(watt_d[l_ * 12 + i], 3072) for i in range(12)]
            plan += [(wrnn_d[l_ * 8 + i], 2304) for i in range(8)] * 2
            for hf in range(2):
                plan += [(wmrg_d[l_ * 8 + hf * 4 + i], 3584) for i in range(4)]
                plan += [(wout_d[l_ * 2 + hf], 4096)]
            for hf in range(2):
                plan += [(wup_d[l_ * 24 + hf * 12 + i], 2048) for i in range(12)]
                plan += [(wdn_d[(l_ * 2 + hf) * 8 + i], 1536) for i in range(8)]

        def wissue(k):
            src, n = plan[k]
            s = k % 2
            p.dma('gpsimd', 'w%d' % s, lambda e: e.dma_start(out=wsl[s][:, 0:n], in_=src), writes=[('w', s)])

        def wload(src, n):
            i = state['w']
            state['w'] += 1
            assert plan[i][1] == n, (i, plan[i][1], n)
            if i == 0:
                wissue(0)
            if i + 1 < len(plan):
                wissue(i + 1)
            return i % 2

        def gtt(out, in0, in1, op, reads, writes):
            p.op('gpsimd', lambda e: e.tensor_tensor(out=out, in0=in0, in1=in1, op=op), reads, writes)

        def gcopy(out, in_, reads, writes):
            p.op('gpsimd', lambda e: e.tensor_copy(out=out, in_=in_), reads, writes)

        def a16(off, n):
            return arena[:, off // 2: off // 2 + n]

        def a32(off, n):
            return arena32[:, off // 4: off // 4 + n]

        def y16(off, n):
            return ybuf[:, off // 2: off // 2 + n]

        def y32(off, n):
            return ybuf32[:, off // 4: off // 4 + n]

        HALL = [('h', i) for i in range(4)]
        tcs = [slice(i * 512, (i + 1) * 512) for i in range(4)]

        def hsl(tc):
            return slice(PADC + tc * 512, PADC + (tc + 1) * 512)

        p.dma('sync', 'vecs', lambda e: e.dma_start(out=vecs[:], in_=vec_d), writes=['vecs'])
        p.dma('gpsimd', 'cst', lambda e: e.dma_start(out=cst[:], in_=cst_d), writes=['cst'])
        for c in range(8):
            p.dma('sync', 'xin%d' % c, lambda e, c=c: e.dma_start(out=xT[:, c, :], in_=x_d[c]),
                  writes=[('x', c, t) for t in range(4)])
        p.op('vector', lambda e: e.memset(hpad[:, :, 0:PADC], 0.0), [], ['hpz'])
        posi = arena.bitcast(I32)[:, 0:NT]
        ang = a32(8192, NT)
        t_a = a32(16384, NT)
        t_b = a32(24576, NT)
        t_k = arena.bitcast(I32)[:, 8192:8192 + NT]
        pos_b = bass.AP(pos_d.tensor, 0, [[0, 128], [1, NT]])
        p.dma('sync', 'pos', lambda e: e.dma_start(out=posi, in_=pos_b), writes=['posi'])
        vcopy(ang, posi, ['posi'], ['ang'])
        invf = vecs[:, L * NV:L * NV + 1]
        vts(ang, ang, invf, None, ALU.mult, None, ['ang', 'vecs'], ['ang'])
        for which, shift in ((0, np.pi / 2), (1, 0.0)):
            vts(t_a, ang, 1.0 / TWO_PI, shift / TWO_PI + 0.5, ALU.mult, ALU.add, ['ang'], ['t_a'])
            vcopy(t_k, t_a, ['t_a'], ['t_k'])
            vcopy(t_a, t_k, ['t_k'], ['t_a'])
            vts(t_b, ang, float(shift), None, ALU.add, None, ['ang'], ['t_b'])
            vstt(t_b, t_a, -CW1, t_b, ALU.mult, ALU.add, ['t_a', 't_b'], ['t_b'])
            vstt(t_b, t_a, -CW2, t_b, ALU.mult, ALU.add, ['t_a', 't_b'], ['t_b'])
            vts(t_a, t_b, -np.pi, TWO_PI, ALU.is_lt, ALU.mult, ['t_b'], ['t_a'])
            vtt(t_b, t_b, t_a, ALU.add, ['t_a', 't_b'], ['t_b'])
            vts(t_a, t_b, np.pi, -TWO_PI, ALU.is_gt, ALU.mult, ['t_b'], ['t_a'])
            vtt(t_b, t_b, t_a, ALU.add, ['t_a', 't_b'], ['t_b'])
            vts(t_b, t_b, -3.1415925, 3.1415925, ALU.max, ALU.min, ['t_b'], ['t_b'])
            act(t_b, t_b, AF.Sin, ['t_b'], ['t_b'])
            p.dma('sync', 'cs%d' % which, lambda e, which=which: e.dma_start(out=cs_d[which], in_=t_b),
                  reads=['t_b'], writes=[('csd', which)])
        for l in range(L):
            vb, db = l * NV, l * NDV
            vts(dvec[:, db + D_HBA:db + D_HBA + 8], vecs[:, vb + V_BA:vb + V_BA + 8], 0.5, None, ALU.mult, None, ['vecs'], ['dvec'])
            vts(dvec[:, db + D_HBX:db + D_HBX + 8], vecs[:, vb + V_BX:vb + V_BX + 8], 0.5, None, ALU.mult, None, ['vecs'], ['dvec'])
            act(dvec[:, db + D_C8:db + D_C8 + 8], vecs[:, vb + V_LAM:vb + V_LAM + 8], AF.Exp, ['vecs'], ['dvec'], scale=-1.0)
            act(dvec[:, db + D_C8:db + D_C8 + 8], dvec[:, db + D_C8:db + D_C8 + 8], AF.Ln, ['dvec'], ['dvec'], bias=1.0)
            vts(dvec[:, db + D_C8H:db + D_C8H + 8], dvec[:, db + D_C8:db + D_C8 + 8], -4.0, None, ALU.mult, None, ['dvec'], ['dvec'])
            vts(dvec[:, db + D_C8:db + D_C8 + 8], dvec[:, db + D_C8:db + D_C8 + 8], -8.0, None, ALU.mult, None, ['dvec'], ['dvec'])
        p.barrier()

        def norm_phase(gcol):
            sq = [a16(0, 512), a16(1024, 512)]
            lnv = [a32(2048, 512), a32(4096, 512)]
            for tc in range(4):
                b = psn()
                for c in range(8):
                    act(sq[c % 2], xT[:, c, tcs[tc]], AF.Square, [('x', c, tc)], [('sq', c % 2)])
                    mm(ps[:, b, :], ones, sq[c % 2], c == 0, c == 7, [('sq', c % 2), 'cst'], [('ps', b)])
                act(lnv[tc % 2], ps[:, b, :], AF.Ln, [('ps', b)], [('lnv', tc % 2)], scale=1.0 / 1024, bias=EPS)
                act(lnv[tc % 2], lnv[tc % 2], AF.Exp, [('lnv', tc % 2)], [('lnv', tc % 2)], scale=-0.5)
                for c in range(8):
                    vstt(hpad[:, c, hsl(tc)], xT[:, c, tcs[tc]], vecs[:, gcol + c:gcol + c + 1], lnv[tc % 2],
                         ALU.mult, ALU.mult, [('x', c, tc), ('lnv', tc % 2), 'vecs'], [('h', tc)])

        def run_pipeline(units):
            n = len(units)
            maxs = max(len(u) for u in units)
            for t in range(n + maxs - 1):
                for k in range(maxs):
                    ui = t - k
                    if 0 <= ui < n and k < len(units[ui]):
                        units[ui][k]()

        def attn_phase(l):
            vb = l * NV
            Qb = [y16(YR_B + 0, NT), y16(YR_B + 4096, NT)]
            Kb = [y16(YR_B + 8192, NT), y16(YR_B + 12288, NT)]
            Vb = [y16(YR_B + 16384, NT).rearrange("p (t d) -> p t d", t=16),
                  y16(YR_B + 20480, NT).rearrange("p (t d) -> p t d", t=16)]
            accn = y32(YR_B + 24576, NT)
            accd = a32(0, NT)
            cosT = a32(8192, NT)
            sinT = a32(16384, NT)
            sq = [a16(24576, 512), a16(25600, 512)]
            qn = [a16(26624, 512), a16(27648, 512)]
            lnv = [a32(28672, 512), a32(30720, 512)]
            t1 = [a32(32768, 512), a32(34816, 512)]
            t2 = a32(36864, 512)
            pt = [a16(38912, 256), a16(39424, 256), a16(39936, 256), a16(40448, 256)]
            p.dma('sync', 'csl0', lambda e: e.dma_start(out=cosT, in_=cs_d[0]), reads=[('csd', 0)], writes=['cos'])
            p.dma('sync', 'csl1', lambda e: e.dma_start(out=sinT, in_=cs_d[1]), reads=[('csd', 1)], writes=['sin'])
            cnt = {'pu': 0, 'au': 0, 'pair': 0, 'nu': 0}
            heads = [(j, g) for j in range(4) for g in range(3)]

            def proj_units(hi):
                j, g = heads[hi]
                h = 4 * g + j
                d = DIL[g]
                M = NT // d
                nb = M // 128
                hb = hi % 2
                box = {}
                units = []

                def load():
                    s = wload(watt_d[l * 12 + hi], 3072)
                    box['s'] = s
                    box['ws'] = wsl[s][:, 0:3072].rearrange("p (k n) -> p k n", k=8)

                first = [True]
                for which, dst, gcol in ((0, Qb[hb], V_GQ), (1, Kb[hb], V_GK)):
                    dkey = ('Q' if which == 0 else 'K', hb)
                    for tc in range(4):
                        u = cnt['pu'] % 2
                        cnt['pu'] += 1
                        b = u
                        do_load = first[0]
                        first[0] = False

                        def s0(which=which, tc=tc, u=u, b=b, do_load=do_load):
                            p.tag = 'A.s0'
                            if do_load:
                                load()
                            s, ws = box['s'], box['ws']
                            for k in range(8):
                                mm(ps[:, b, :], ws[:, k, which * 128:(which + 1) * 128], hpad[:, k, hsl(tc)], k == 0, k == 7,
                                   [('w', s), ('h', tc)], [('ps', b)])
                            act(sq[u], ps[:, b, :], AF.Square, [('ps', b)], [('sq', u)])

                        def s1(u=u, b=b, gcol=gcol):
                            p.tag = 'A.s1'
                            mm(ps[:, 2, :], ones, sq[u], True, True, [('sq', u), 'cst'], [('ps', 2)])
                            act(lnv[u], ps[:, 2, :], AF.Ln, [('ps', 2)], [('lnv', u)], scale=1.0 / 128, bias=EPS)
                            act(lnv[u], lnv[u], AF.Exp, [('lnv', u)], [('lnv', u)], scale=-0.5)
                            vstt(qn[u], ps[:, b, :], vecs[:, vb + gcol + h:vb + gcol + h + 1], lnv[u], ALU.mult, ALU.mult,
                                 [('ps', b), ('lnv', u), 'vecs'], [('qn', u)])

                        def s2(u=u, tc=tc, dst=dst, dkey=dkey):
                            p.tag = 'A.s2'
                            mm(ps[:, 3, :], PT, qn[u], True, True, [('qn', u), 'cst'], [('ps', 3)])
                            gtt(t1[u], qn[u], cosT[:, tcs[tc]], ALU.mult, [('qn', u), 'cos'], [('t1', u)])
                            vtt(t2, ps[:, 3, :], sinT[:, tcs[tc]], ALU.mult, [('ps', 3), 'sin'], ['t2'])
                            if d == 1:
                                o_ap = dst[:, tcs[tc]]
                                i0, i1 = t1[u], t2
                            else:
                                m0 = tc * 512 // d
                                o_ap = dst.rearrange("p (r m) -> p m r", r=d)[:, m0:m0 + 512 // d, :]
                                i0 = t1[u].rearrange("p (m r) -> p m r", r=d)
                                i1 = t2.rearrange("p (m r) -> p m r", r=d)
                            gtt(o_ap, i0, i1, ALU.add, [('t1', u), 't2'], [dkey])

                        units.append([s0, s1, s2])
                for vg in range(4):
                    u = cnt['pu'] % 2
                    cnt['pu'] += 1

                    def v0(vg=vg, b=u):
                        p.tag = 'A.v0'
                        s, ws = box['s'], box['ws']
                        for ti in range(vg * 4, vg * 4 + 4):
                            r, bb = divmod(ti, nb)
                            st0 = PADC + r + d * bb * 128
                            for k in range(8):
                                mm(ps[:, b, (ti % 4) * 128:(ti % 4 + 1) * 128], hpad[:, k, st0:st0 + d * 127 + 1:d],
                                   ws[:, k, 256:384], k == 0, k == 7, [('w', s)] + HALL, [('ps', b)])
                        act(Vb[hb][:, vg * 4:vg * 4 + 4, :], ps[:, b, :].rearrange("p (t d) -> p t d", t=4), AF.Copy,
                            [('ps', b)], [('V', hb)])

                    units.append([v0])
                return units

            def attn_units(hi):
                j, g = heads[hi]
                d = DIL[g]
                M = NT // d
                nb = M // 128
                hb = hi % 2
                Q, K, V = Qb[hb], Kb[hb], Vb[hb]
                rq = [('Q', hb), ('K', hb)]
                units = []
                for ti in range(16):
                    r, bb = divmod(ti, nb)
                    qc = slice(ti * 128, ti * 128 + 128)
                    pc = slice((ti - 1) * 128, ti * 128)
                    au = cnt['au']
                    cnt['au'] += 1
                    bs = 4 + au % 2
                    pi_ = au % 4
                    if ti % 2 == 0:
                        cnt['pair'] += 1
                    bo = 6 + cnt['pair'] % 2
                    half = ti % 2
                    oc = slice(half * 128, half * 128 + 128)
                    dc = slice(256 + half * 128, 256 + half * 128 + 128)

                    def t0(bb=bb, qc=qc, pc=pc, bs=bs, pi_=pi_):
                        p.tag = 'A.t0'
                        if bb > 0:
                            mm(ps[:, bs, 0:256], ident, maskb[:, 0:256], True, False, ['cst'], [('ps', bs)])
                            mm(ps[:, bs, 0:128], K[:, pc], Q[:, qc], False, False, rq, [('ps', bs)])
                            mm(ps[:, bs, 128:256], K[:, qc], Q[:, qc], False, True, rq, [('ps', bs)])
                            act(pt[pi_], ps[:, bs, 0:256], AF.Exp, [('ps', bs)], [('pt', pi_)], scale=SC)
                        else:
                            mm(ps[:, bs, 0:128], ident, maskb[:, 128:256], True, False, ['cst'], [('ps', bs)])
                            mm(ps[:, bs, 0:128], K[:, qc], Q[:, qc], False, True, rq, [('ps', bs)])
                            act(pt[pi_][:, 0:128], ps[:, bs, 0:128], AF.Exp, [('ps', bs)], [('pt', pi_)], scale=SC)

                    def t1_(ti=ti, bb=bb, bo=bo, oc=oc, dc=dc, pi_=pi_, half=half):
                        p.tag = 'A.t1'
                        rv = [('V', hb), ('pt', pi_)]
                        rc = ['cst', ('pt', pi_)]
                        if bb > 0:
                            mm(ps[:, bo, oc], V[:, ti - 1, :], pt[pi_][:, 0:128], True, False, rv, [('ps', bo)])
                            mm(ps[:, bo, oc], V[:, ti, :], pt[pi_][:, 128:256], False, True, rv, [('ps', bo)])
                            mm(ps[:, bo, dc], ones, pt[pi_][:, 0:128], True, False, rc, [('ps', bo)])
                            mm(ps[:, bo, dc], ones, pt[pi_][:, 128:256], False, True, rc, [('ps', bo)])
                        else:
                            mm(ps[:, bo, oc], V[:, ti, :], pt[pi_][:, 0:128], True, True, rv, [('ps', bo)])
                            mm(ps[:, bo, dc], ones, pt[pi_][:, 0:128], True, True, rc, [('ps', bo)])
                        if half == 1:
                            e2 = ti // 2
                            for c0, acc, akey in ((0, accn, 'accn'), (256, accd, 'accd')):
                                src = ps[:, bo, c0:c0 + 256]
                                if g == 0:
                                    dstv = acc[:, e2 * 256:(e2 + 1) * 256]
                                elif g == 1:
                                    rr, hh2 = divmod(e2, 2)
                                    dstv = acc[:, rr + 4 * 256 * hh2:rr + 4 * 256 * hh2 + 4 * 255 + 1:4]
                                else:
                                    dstv = acc.rearrange("p (i r) -> p r i", r=16)[:, 2 * e2:2 * e2 + 2, :]
                                    src = src.rearrange("p (q i) -> p q i", q=2)
                                if g == 0:
                                    vcopy(dstv, src, [('ps', bo)], [akey])
                                else:
                                    vtt(dstv, src, dstv, ALU.add, [('ps', bo), akey], [akey])

                    units.append([t0, t1_])
                if g == 2:
                    for tc in range(4):
                        u = cnt['nu'] % 2
                        cnt['nu'] += 1

                        def n0(tc=tc, u=u):
                            pass

                        def n1(tc=tc, u=u):
                            pass

                        def n2(tc=tc, u=u):
                            p.tag = 'A.n2'
                            act(t1[u], accd[:, tcs[tc]], AF.Ln, ['accd'], [('t1', u)])
                            act(t1[u], t1[u], AF.Exp, [('t1', u)], [('t1', u)], scale=-1.0)
                            vtt(yatt[:, j, tcs[tc]], accn[:, tcs[tc]], t1[u], ALU.mult, ['accn', ('t1', u)], [('yatt', j)])

                        units.append([n0, n1, n2])
                return units

            def interleave(a, b):
                out = []
                na, nb_ = len(a), len(b)
                ia = ib = 0
                while ia < na or ib < nb_:
                    if ib >= nb_ or (ia < na and (ia + 1) * nb_ <= (ib + 1) * na):
                        out.append(a[ia])
                        ia += 1
                    else:
                        out.append(b[ib])
                        ib += 1
                return out

            def steps_of(units, order=(0, 1, 2)):
                n = len(units)
                maxs = max(len(u) for u in units)
                steps = []
                for t in range(n + maxs - 1):
                    st = []
                    for k in order:
                        ui = t - k
                        if 0 <= ui < n and k < len(units[ui]):
                            st.append(units[ui][k])
                    steps.append(st)
                return steps

            for hs in range(13):
                psteps = steps_of(proj_units(hs), (1, 0, 2)) if hs < 12 else []
                asteps = steps_of(attn_units(hs - 1)) if hs >= 1 else []
                for st in interleave(psteps, asteps):
                    for fn in st:
                        fn()

        def rnn_phase(l):
            vb, db = l * NV, l * NDV
            HT = NT // 2
            xc = [a32(0, HT), a32(4096, HT)]
            thr = [a32(8192, HT), a32(12288, HT)]
            gg = [a16(16384, HT), a16(18432, HT)]
            xcb = [a16(20480, 512), a16(21504, 512)]
            av = [a32(22528, 512), a32(24576, 512)]
            a2 = [a32(26624, 512), a32(28672, 512)]
            uu = [a32(30720, 512), a32(32768, 512)]
            hh = [a32(34816, 512), a32(36864, 512)]
            ti_ = [a32(38912, 512), uu[0]]
            tik = [('ti', 0), ('uu', 0)]
            state_r = {'cnt': 0}
            LN_HALF = float(np.log(0.5))

            def front(th, c, q):
                T0 = th * HT
                s = wload(wrnn_d[l * 8 + c], 2304)
                wxg = wsl[s][:, 0:2048].rearrange("p (k n) -> p k n", k=8)
                wa = wsl[s][:, 2048:2176]
                wx = wsl[s][:, 2176:2304]
                hkeys = [('h', 2 * th), ('h', 2 * th + 1)] + ([('h', 2 * th - 1)] if th else ['hpz'])
                for w in range(3):
                    c0 = T0 + 509 * w
                    wl = min(512, T0 + HT + PADC - c0)
                    nv = wl - 3
                    lo = 509 * w
                    b = psn()
                    for k in range(8):
                        mm(ps[:, b, 0:wl], wxg[:, k, 0:128], hpad[:, k, c0:c0 + wl], k == 0, k == 7,
                           [('w', s)] + hkeys, [('ps', b)])
                    vts(xc[q][:, lo:lo + nv], ps[:, b, 3:3 + nv], vecs[:, vb + V_CW + 24 + c:vb + V_CW + 25 + c],
                        vecs[:, vb + V_CB + c:vb + V_CB + c + 1], ALU.mult, ALU.add, [('ps', b), 'vecs'], [('xc', q)])
                    for i in range(3):
                        vstt(xc[q][:, lo:lo + nv], ps[:, b, i:i + nv], vecs[:, vb + V_CW + 8 * i + c:vb + V_CW + 8 * i + c + 1],
                             xc[q][:, lo:lo + nv], ALU.mult, ALU.add, [('ps', b), ('xc', q), 'vecs'], [('xc', q)])
                for t2_ in range(2):
                    tc = 2 * th + t2_
                    ls = slice(t2_ * 512, (t2_ + 1) * 512)
                    b3 = psn()
                    for k in range(8):
                        mm(ps[:, b3, :], wxg[:, k, 128:256], hpad[:, k, hsl(tc)], k == 0, k == 7,
                           [('w', s), ('h', tc)], [('ps', b3)])
                    act(gg[q][:, ls], ps[:, b3, :], AF.Gelu_apprx_tanh, [('ps', b3)], [('gg', q)])
                for t2_ in range(2):
                    ls = slice(t2_ * 512, (t2_ + 1) * 512)
                    vcopy(xcb[t2_], xc[q][:, ls], [('xc', q)], [('xcb', t2_)])
                bks = []
                for t2_ in range(2):
                    b1 = psn()
                    mm(ps[:, b1, :], wa, xcb[t2_], True, True, [('w', s), ('xcb', t2_)], [('ps', b1)])
                    b2 = psn()
                    mm(ps[:, b2, :], wx, xcb[t2_], True, True, [('w', s), ('xcb', t2_)], [('ps', b2)])
                    bks.append((b1, b2))
                for t2_ in range(2):
                    ls = slice(t2_ * 512, (t2_ + 1) * 512)
                    b1, b2 = bks[t2_]
                    act(thr[q][:, ls], ps[:, b1, :], AF.Tanh, [('ps', b1), 'dvec'], [('thr', q)], scale=0.5,
                        bias=dvec[:, db + D_HBA + c:db + D_HBA + c + 1])
                    act(ti_[t2_], ps[:, b2, :], AF.Tanh, [('ps', b2), 'dvec'], [tik[t2_]], scale=0.5,
                        bias=dvec[:, db + D_HBX + c:db + D_HBX + c + 1])
                    vstt(xc[q][:, ls], ti_[t2_], 1.0, xc[q][:, ls], ALU.add, ALU.mult, [tik[t2_], ('xc', q), ('xcb', t2_)], [('xc', q)])

            def back(th, c, q):
                for t2_ in range(2):
                    tc = 2 * th + t2_
                    ls = slice(t2_ * 512, (t2_ + 1) * 512)
                    u = state_r['cnt'] % 2
                    state_r['cnt'] += 1
                    c8 = dvec[:, db + D_C8 + c:db + D_C8 + c + 1]
                    c8h = dvec[:, db + D_C8H + c:db + D_C8H + c + 1]
                    act(av[u], thr[q][:, ls], AF.Exp, [('thr', q), 'dvec'], [('av', u)], scale=c8h, bias=c8h)
                    act(a2[u], thr[q][:, ls], AF.Exp, [('thr', q), 'dvec'], [('a2', u)], scale=c8, bias=c8)
                    act(a2[u], a2[u], AF.Ln, [('a2', u)], [('a2', u)], scale=-1.0, bias=1.0)
                    act(a2[u], a2[u], AF.Exp, [('a2', u)], [('a2', u)], scale=0.5, bias=LN_HALF)
                    gtt(uu[u], xc[q][:, ls], a2[u], ALU.mult, [('xc', q), ('a2', u)], [('uu', u)])
                    if t2_ == 0:
                        init = 0.0 if th == 0 else hst[:, c:c + 1]
                        rd = [('av', u), ('uu', u)] + ([] if th == 0 else [('hst', c)])
                    else:
                        init = hh[1 - u][:, 511:512]
                        rd = [('av', u), ('uu', u), ('hh', 1 - u)]
                    p.op('vector', lambda e, u=u, init=init: e.tensor_tensor_scan(
                        out=hh[u], data0=av[u], data1=uu[u], initial=init, op0=ALU.mult, op1=ALU.add), rd, [('hh', u)])
                    if th == 0 and t2_ == 1:
                        vcopy(hst[:, c:c + 1], hh[u][:, 511:512], [('hh', u)], [('hst', c)])
                    gtt(yrnn[:, c, tcs[tc]], hh[u], gg[q][:, ls], ALU.mult, [('hh', u), ('gg', q)], [('yrnn', c)])

            units = [(th, c) for th in range(2) for c in range(8)]
            front(units[0][0], units[0][1], 0)
            for n in range(16):
                if n + 1 < 16:
                    front(units[n + 1][0], units[n + 1][1], (n + 1) % 2)
                back(units[n][0], units[n][1], n % 2)

        def merge_phase(l):
            mg = a16(0, 4 * NT).rearrange("p (c t) -> p c t", c=4)
            th1 = [a32(16384, 512), a32(18432, 512)]
            th2 = [a32(20480, 512), a32(22528, 512)]
            m1 = [a32(24576, 512), a32(26624, 512)]
            cnt = 0
            YALL = [('yrnn', c) for c in range(8)]
            AALL = [('yatt', j) for j in range(4)]
            for half in range(2):
                for ci in range(4):
                    c = half * 4 + ci
                    s = wload(wmrg_d[l * 8 + c], 3584)
                    prn = wsl[s][:, 0:1024].rearrange("p (k n) -> p k n", k=8)
                    pat = wsl[s][:, 1024:1536].rearrange("p (k n) -> p k n", k=4)
                    wg = wsl[s][:, 1536:3584].rearrange("p (k n) -> p k n", k=8)
                    for tc in range(4):
                        u = cnt % 2
                        cnt += 1
                        bA, bB, bG1, bG2 = psn(), psn(), psn(), psn()
                        for k in range(8):
                            mm(ps[:, bG1, :], wg[:, k, 0:128], hpad[:, k, hsl(tc)], k == 0, k == 7, [('w', s), ('h', tc)], [('ps', bG1)])
                        for k in range(8):
                            mm(ps[:, bG2, :], wg[:, k, 128:256], hpad[:, k, hsl(tc)], k == 0, k == 7, [('w', s), ('h', tc)], [('ps', bG2)])
                        for k in range(8):
                            mm(ps[:, bA, :], prn[:, k, :], yrnn[:, k, tcs[tc]], k == 0, k == 7, [('w', s)] + YALL, [('ps', bA)])
                        for k in range(4):
                            mm(ps[:, bB, :], pat[:, k, :], yatt[:, k, tcs[tc]], k == 0, k == 3, [('w', s)] + AALL, [('ps', bB)])
                        act(th1[u], ps[:, bG1, :], AF.Tanh, [('ps', bG1)], [('th1', u)], scale=0.5)
                        act(th2[u], ps[:, bG2, :], AF.Tanh, [('ps', bG2)], [('th2', u)], scale=0.5)
                        vstt(m1[u], th1[u], 1.0, ps[:, bA, :], ALU.add, ALU.mult, [('th1', u), ('ps', bA)], [('m1', u)])
                        vstt(th2[u], th2[u], 1.0, ps[:, bB, :], ALU.add, ALU.mult, [('th2', u), ('ps', bB)], [('th2', u)])
                        vtt(mg[:, ci, tcs[tc]], m1[u], th2[u], ALU.add, [('m1', u), ('th2', u)], [('mg', ci)])
                s = wload(wout_d[l * 2 + half], 4096)
                wo = wsl[s][:, 0:4096].rearrange("p (c k n) -> p c k n", c=8, k=4)
                for c2 in range(8):
                    for tc in range(4):
                        b = psn()
                        for ci in range(4):
                            mm(ps[:, b, :], wo[:, c2, ci, :], mg[:, ci, tcs[tc]], ci == 0, ci == 3,
                               [('w', s)] + [('mg', i) for i in range(4)], [('ps', b)])
                        vstt(xT[:, c2, tcs[tc]], ps[:, b, :], 0.5, xT[:, c2, tcs[tc]], ALU.mult, ALU.add,
                             [('ps', b), ('x', c2, tc)], [('x', c2, tc)])

        def ffn_phase(l):
            vb = l * NV
            acth = ybuf[:, 0:12 * NT].rearrange("p (f t) -> p f t", f=12)
            ug = [a32(0, 512), a32(2048, 512)]
            uv = [a32(4096, 512), a32(6144, 512)]
            cnt = 0
            for half in range(2):
                for fi in range(12):
                    f = half * 12 + fi
                    s = wload(wup_d[l * 24 + f], 2048)
                    wu = wsl[s][:, 0:2048].rearrange("p (k n) -> p k n", k=8)
                    for w in range(5):
                        c0 = 1 + 510 * w
                        wl = min(512, PADC + NT - c0)
                        nv = wl - 2
                        u = cnt % 2
                        cnt += 1
                        bg, bv = psn(), psn()
                        for k in range(8):
                            mm(ps[:, bg, 0:wl], wu[:, k, 0:128], hpad[:, k, c0:c0 + wl], k == 0, k == 7,
                               [('w', s), 'hpz'] + HALL, [('ps', bg)])
                        for k in range(8):
                            mm(ps[:, bv, 0:wl], wu[:, k, 128:256], hpad[:, k, c0:c0 + wl], k == 0, k == 7,
                               [('w', s), 'hpz'] + HALL, [('ps', bv)])
                        for bank, dst, key, fc in ((bg, ug[u], ('ug', u), f), (bv, uv[u], ('uv', u), 24 + f)):
                            def wcol(i, fc=fc):
                                return vecs[:, vb + V_FW + 48 * i + fc:vb + V_FW + 48 * i + fc + 1]
                            act(dst[:, 0:nv], ps[:, bank, 2:2 + nv], AF.Identity, [('ps', bank), 'vecs'], [key],
                                scale=wcol(2), bias=vecs[:, vb + V_FB + fc:vb + V_FB + fc + 1])
                            for i in range(2):
                                vstt(dst[:, 0:nv], ps[:, bank, i:i + nv], wcol(i), dst[:, 0:nv], ALU.mult, ALU.add,
                                     [('ps', bank), key, 'vecs'], [key])
                        act(ug[u][:, 0:nv], ug[u][:, 0:nv], AF.Gelu_apprx_tanh, [('ug', u)], [('ug', u)])
                        gtt(acth[:, fi, 510 * w:510 * w + nv], ug[u][:, 0:nv], uv[u][:, 0:nv], ALU.mult,
                            [('ug', u), ('uv', u)], [('act', fi)])
                for c in range(8):
                    s = wload(wdn_d[(l * 2 + half) * 8 + c], 1536)
                    wd = wsl[s][:, 0:1536].rearrange("p (k n) -> p k n", k=12)
                    for tc in range(4):
                        b = psn()
                        for fi in range(12):
                            mm(ps[:, b, :], wd[:, fi, :], acth[:, fi, tcs[tc]], fi == 0, fi == 11,
                               [('w', s)] + [('act', i) for i in range(12)], [('ps', b)])
                        vtt(xT[:, c, tcs[tc]], ps[:, b, :], xT[:, c, tcs[tc]], ALU.add, [('ps', b), ('x', c, tc)], [('x', c, tc)])

        for l in range(L):
            p.tag = 'N1'
            norm_phase(l * NV + V_G1)
            attn_phase(l)
            p.barrier()
            p.tag = 'R'
            rnn_phase(l)
            p.barrier()
            p.tag = 'M'
            merge_phase(l)
            p.barrier()
            p.tag = 'N2'
            norm_phase(l * NV + V_G2)
            p.tag = 'F'
            ffn_phase(l)
            p.barrier()
        for c in range(8):
            p.dma('sync', 'yout%d' % c, lambda e, c=c: e.dma_start(out=y_d[c], in_=xT[:, c, :]),
                  reads=[('x', c, t) for t in range(4)])
        p.finish('sync')
        p.emit(es)
        import os as _os
        if _os.environ.get('MK_TAGS'):
            import json as _json
            _json.dump(p.tags, open(_os.environ['MK_TAGS'], 'w'))
    return nc


def _chunk_k(W):
    K, N = W.shape
    return W.reshape(K // 128, 128, N).transpose(1, 0, 2)


def _colvec(v, n):
    return np.asarray(v).reshape(n, 128).T


def _consts():
    c = np.zeros((128, 640), np.float32)
    c[:, 0:128] = np.eye(128, dtype=np.float32)
    c[:, 128:256] = 1.0
    for m in range(128):
        if m < 64:
            c[m + 64, 256 + m] = -1.0
        else:
            c[m - 64, 256 + m] = 1.0
    i = np.arange(128)[:, None]
    j = np.arange(128)[None, :]
    c[:, 384:512] = np.where(i >= j, 0.0, NEG)
    c[:, 512:640] = np.where(i <= j, 0.0, NEG)
    return c


def _prep_weights(L, w_in, lru_wa, lru_wx, proj_rnn, proj_attn, w_out, w_up, w_down):
    watt = np.empty((L * 12, 128, 3072), np.float32)
    wrnn = np.empty((L * 8, 128, 2304), np.float32)
    wmrg = np.empty((L * 8, 128, 3584), np.float32)
    wout = np.empty((L * 2, 128, 4096), np.float32)
    wup = np.empty((L * 24, 128, 2048), np.float32)
    wdn = np.empty((L * 16, 128, 1536), np.float32)
    for l in range(L):
        wi = w_in[l]
        for j in range(4):
            for g in range(3):
                h = 4 * g + j
                cols = np.concatenate([wi[:, 2048 + h * 128:2048 + (h + 1) * 128],
                                       wi[:, 3584 + h * 128:3584 + (h + 1) * 128],
                                       wi[:, 5120 + h * 128:5120 + (h + 1) * 128]], axis=1)
                watt[l * 12 + j * 3 + g] = _chunk_k(cols).reshape(128, 3072)
        for c in range(8):
            cols = np.concatenate([wi[:, c * 128:(c + 1) * 128], wi[:, 1024 + c * 128:1024 + (c + 1) * 128]], axis=1)
            wrnn[l * 8 + c, :, 0:2048] = _chunk_k(cols).reshape(128, 2048)
            wrnn[l * 8 + c, :, 2048:2176] = lru_wa[l, c]
            wrnn[l * 8 + c, :, 2176:2304] = lru_wx[l, c]
            wmrg[l * 8 + c, :, 0:1024] = _chunk_k(proj_rnn[l][:, c * 128:(c + 1) * 128]).reshape(128, 1024)
            wmrg[l * 8 + c, :, 1024:1536] = _chunk_k(proj_attn[l][:, c * 128:(c + 1) * 128]).reshape(128, 512)
            gcols = np.concatenate([wi[:, 6656 + c * 128:6656 + (c + 1) * 128], wi[:, 7680 + c * 128:7680 + (c + 1) * 128]], axis=1)
            wmrg[l * 8 + c, :, 1536:3584] = _chunk_k(gcols).reshape(128, 2048)
        for half in range(2):
            blk = w_out[l][half * 512:(half + 1) * 512, :]
            a = blk.reshape(4, 128, 8, 128).transpose(1, 2, 0, 3)
            wout[l * 2 + half] = a.reshape(128, 4096)
        for f in range(24):
            cols = np.concatenate([w_up[l][:, f * 128:(f + 1) * 128], w_up[l][:, 3072 + f * 128:3072 + (f + 1) * 128]], axis=1)
            wup[l * 24 + f] = _chunk_k(cols).reshape(128, 2048)
        for half in range(2):
            for c in range(8):
                blk = w_down[l][half * 1536:(half + 1) * 1536, c * 128:(c + 1) * 128]
                wdn[(l * 2 + half) * 8 + c] = blk.reshape(12, 128, 128).transpose(1, 0, 2).reshape(128, 1536)
    return watt, wrnn, wmrg, wout, wup, wdn


def _prep_vecs(L, ln1_g, ln2_g, rnn_conv_w, rnn_conv_b, lru_ba, lru_bx, lru_lambda, q_norm_g, k_norm_g, ffn_conv_w, ffn_conv_b):
    out = np.zeros((128, L * NV + 1), np.float32)
    for l in range(L):
        parts = [_colvec(ln1_g[l], 8), _colvec(ln2_g[l], 8)]
        parts += [_colvec(rnn_conv_w[l][i], 8) for i in range(4)]
        parts += [_colvec(rnn_conv_b[l], 8), _colvec(lru_ba[l], 8), _colvec(lru_bx[l], 8), _colvec(lru_lambda[l], 8)]
        parts += [np.asarray(q_norm_g[l]).T, np.asarray(k_norm_g[l]).T]
        parts += [_colvec(ffn_conv_w[l][i], 48) for i in range(3)]
        parts += [_colvec(ffn_conv_b[l], 48)]
        out[:, l * NV:(l + 1) * NV] = np.concatenate(parts, axis=1)
    half = 64
    inv_freq = (np.float32(10000.0) ** (-np.arange(half, dtype=np.float32) / np.float32(half))).astype(np.float32)
    out[:, L * NV] = np.concatenate([inv_freq, inv_freq])
    return out


_NC_CACHE = {}


def _get_nc(L):
    if L not in _NC_CACHE:
        _NC_CACHE[L] = build(L)
    return _NC_CACHE[L]


def run_layers(x, positions, L, layer0, params, core_ids=None):
    B = x.shape[0]
    sl = slice(layer0, layer0 + L)
    f = lambda a: np.ascontiguousarray(np.asarray(a, dtype=np.float32)[sl])
    watt, wrnn, wmrg, wout, wup, wdn = _prep_weights(L, f(params['w_in']), f(params['lru_wa']), f(params['lru_wx']),
                                                    f(params['proj_rnn']), f(params['proj_attn']), f(params['w_out']),
                                                    f(params['w_up']), f(params['w_down']))
    vecs = _prep_vecs(L, f(params['ln1_g']), f(params['ln2_g']), f(params['rnn_conv_w']), f(params['rnn_conv_b']),
                      f(params['lru_ba']), f(params['lru_bx']), f(params['lru_lambda']), f(params['q_norm_g']),
                      f(params['k_norm_g']), f(params['ffn_conv_w']), f(params['ffn_conv_b']))
    cst = _consts()
    nc = _get_nc(L)
    in_maps = []
    for b in range(B):
        xt = np.ascontiguousarray(np.asarray(x[b], np.float32).T).reshape(8, 128, NT)
        in_maps.append({"x": xt, "pos": np.ascontiguousarray(np.asarray(positions[b], np.int32).reshape(1, NT)),
                        "vecs": vecs, "cst": cst, "w_att": watt, "w_rnn": wrnn, "w_mrg": wmrg, "w_out": wout,
                        "w_up": wup, "w_dn": wdn})
    res = run_bass_kernel_spmd(nc, in_maps, core_ids=list(range(B)) if core_ids is None else core_ids)
    out = np.empty((B, NT, 1024), np.float32)
    for b in range(B):
        out[b] = np.asarray(res.results[b]["y"], np.float32).reshape(1024, NT).T
    return out


def kernel(x, positions, ln1_g, w_in, rnn_conv_w, rnn_conv_b, lru_wa, lru_ba, lru_wx, lru_bx,
           lru_lambda, q_norm_g, k_norm_g, proj_rnn, proj_attn, w_out, ln2_g, w_up,
           ffn_conv_w, ffn_conv_b, w_down):
    params = dict(ln1_g=ln1_g, w_in=w_in, rnn_conv_w=rnn_conv_w, rnn_conv_b=rnn_conv_b, lru_wa=lru_wa, lru_ba=lru_ba,
                  lru_wx=lru_wx, lru_bx=lru_bx, lru_lambda=lru_lambda, q_norm_g=q_norm_g, k_norm_g=k_norm_g,
                  proj_rnn=proj_rnn, proj_attn=proj_attn, w_out=w_out, ln2_g=ln2_g, w_up=w_up,
                  ffn_conv_w=ffn_conv_w, ffn_conv_b=ffn_conv_b, w_down=w_down)
    return run_layers(np.asarray(x), np.asarray(positions), DEPTH, 0, params)
```
